# Optimizing a Trainium2 kernel written in Bass

```python
import math
import jax, jax.numpy as jnp
from jax import lax
import numpy as np

D_MODEL = 1024
BATCH = 2
SEQ = 8192
DEPTH = 2
DEC_BATCH = 128
DEC_SEQ = 4
PAST_LEN = 2048
PAGE_SIZE = 128

HEAD_DIM = 64
H_A = 8
BRANCHES = ((128, 1), (512, 4), (2048, 16))
MAX_WINDOW = 2048
ATT_BLOCK = 128
H_B = 4
DK_B = 32
DV_B = 64
GATE_RANK = 16
GATE_TAU = 16.0
H_C = 4
DK_C = 32
DV_C = 64
LA_CHUNK = 16
W_A = H_A * HEAD_DIM
QK_B = H_B * DK_B
V_B = H_B * DV_B
QK_C = H_C * DK_C
V_C = H_C * DV_C
MIX_WIDTH = W_A + V_B + V_C
D_IN = 3 * W_A + 2 * QK_B + 2 * V_B + GATE_RANK + 2 * QK_C + 2 * V_C
N_KEYS = 128
N_EXPERTS = N_KEYS * N_KEYS
PEER_HEADS = 8
PEER_DKEY = 256
PEER_TOPK = 16
PEER_BLOCK = 128
NORM_EPS = 1e-6

kernel_name = "hymba_dilated_gla_retnet_peer_step"


def rms_norm(x, g):
    xf = x.astype(jnp.float32)
    y = xf * lax.rsqrt(jnp.mean(xf * xf, axis=-1, keepdims=True) + NORM_EPS)
    return (y * g.astype(jnp.float32)).astype(x.dtype)


def alibi_slopes():
    return 2.0 ** (-8.0 * jnp.arange(1, H_A + 1, dtype=jnp.float32) / H_A)


def split_columns(z):
    sizes = (W_A, W_A, W_A, QK_B, QK_B, V_B, V_B, GATE_RANK, QK_C, QK_C, V_C, V_C)
    out, start = [], 0
    for n in sizes:
        out.append(z[..., start:start + n])
        start += n
    return out


def dilated_branch_prompt(q, k, v, window, dilation, slopes):
    B, S, H, Dh = q.shape
    L = S // dilation
    steps = window // dilation
    nb = -(-L // ATT_BLOCK)
    Lp = nb * ATT_BLOCK

    def to_blocks(t):
        t = t.reshape(B, L, dilation, H, Dh).transpose(0, 2, 1, 3, 4)
        t = jnp.pad(t, ((0, 0), (0, 0), (0, Lp - L), (0, 0), (0, 0)))
        return t.reshape(B, dilation, nb, ATT_BLOCK, H, Dh)

    def with_prev(t):
        prev = jnp.pad(t, ((0, 0), (0, 0), (1, 0), (0, 0), (0, 0), (0, 0)))[:, :, :-1]
        return jnp.concatenate([prev, t], axis=3)

    qb = to_blocks(q)
    kk = with_prev(to_blocks(k))
    vv = with_prev(to_blocks(v))
    s = jnp.einsum('brnqhd,brnkhd->brnhqk', qb, kk, preferred_element_type=jnp.float32) * (Dh ** -0.5)
    qi = jnp.arange(ATT_BLOCK)[:, None]
    ki = jnp.arange(2 * ATT_BLOCK)[None, :]
    dist = qi + ATT_BLOCK - ki
    key_elem = jnp.arange(nb)[:, None, None] * ATT_BLOCK - ATT_BLOCK + ki[None]
    valid = (dist >= 0) & (dist <= steps) & (key_elem >= 0)
    bias = -slopes[:, None, None] * (dist * dilation).astype(jnp.float32)[None]
    s = jnp.where(valid[:, None], s + bias, -jnp.inf)
    m = jnp.max(s, axis=-1, keepdims=True)
    p = jnp.exp(s - m)
    den = jnp.sum(p, axis=-1, keepdims=True)
    o = jnp.einsum('brnhqk,brnkhd->brnqhd', p / den, vv)
    lse = (m + jnp.log(den))[..., 0]
    o = o.reshape(B, dilation, Lp, H, Dh)[:, :, :L].transpose(0, 2, 1, 3, 4).reshape(B, S, H, Dh)
    lse = lse.transpose(0, 1, 2, 4, 3).reshape(B, dilation, Lp, H)[:, :, :L]
    lse = lse.transpose(0, 2, 1, 3).reshape(B, S, H)
    return o, lse


def dilated_branch_sample(q, k_all, v_all, window, dilation, slopes):
    Bd, T, H, Dh = q.shape
    n_past = k_all.shape[1] - T
    j = jnp.arange(window // dilation + 1)
    key_idx = n_past + jnp.arange(T)[:, None] - j[None, :] * dilation
    valid = key_idx >= 0
    idx = jnp.maximum(key_idx, 0)
    kg = jnp.take(k_all, idx, axis=1)
    vg = jnp.take(v_all, idx, axis=1)
    s = jnp.einsum('bthd,btjhd->bthj', q, kg, preferred_element_type=jnp.float32) * (Dh ** -0.5)
    s = s - slopes[:, None] * (j * dilation).astype(jnp.float32)[None, :]
    s = jnp.where(valid[:, None, :], s, -jnp.inf)
    m = jnp.max(s, axis=-1, keepdims=True)
    p = jnp.exp(s - m)
    den = jnp.sum(p, axis=-1, keepdims=True)
    o = jnp.einsum('bthj,btjhd->bthd', p / den, vg)
    return o, (m + jnp.log(den))[..., 0]


def combine_branches(outs, lses):
    w = jax.nn.softmax(jnp.stack(lses, axis=0), axis=0)
    return jnp.einsum('gbth,gbthd->bthd', w, jnp.stack(outs, axis=0))


def chunked_gated_linear_attn(q, k, v, log_a, s0):
    B, T, H, Dk = q.shape
    Dv = v.shape[-1]
    C = LA_CHUNK
    N = -(-T // C)
    pad = N * C - T

    def prep(t):
        t = jnp.pad(t.astype(jnp.float32), ((0, 0), (0, pad), (0, 0), (0, 0)))
        return t.reshape(B, N, C, H, t.shape[-1])

    qc, kc, vc, la = prep(q), prep(k), prep(v), prep(log_a)
    b = jnp.cumsum(la, axis=2)
    causal = jnp.tril(jnp.ones((C, C), dtype=bool))
    diff = b[:, :, :, None] - b[:, :, None, :]
    decay = jnp.exp(jnp.where(causal[None, None, :, :, None, None], diff, -jnp.inf))
    att = jnp.einsum('bnihd,bnjhd,bnijhd->bnhij', qc, kc, decay)
    o_intra = jnp.einsum('bnhij,bnjhe->bnihe', att, vc)
    b_last = b[:, :, -1]
    kv = jnp.einsum('bnjhd,bnjhe->bnhde', kc * jnp.exp(b_last[:, :, None] - b), vc)

    def step(S, inp):
        kv_n, bl = inp
        return jnp.exp(bl)[..., None] * S + kv_n, S

    s_fin, s_before = lax.scan(step, s0.astype(jnp.float32),
                               (kv.transpose(1, 0, 2, 3, 4), b_last.transpose(1, 0, 2, 3)))
    s_before = s_before.transpose(1, 0, 2, 3, 4)
    o_inter = jnp.einsum('bnihd,bnhde->bnihe', qc * jnp.exp(b), s_before)
    o = (o_intra + o_inter).reshape(B, N * C, H, Dv)[:, :T]
    return o, s_fin


def token_mixer(h, w_in, w_gate2, b_gate, g_gla, w_out, kv_past, s_gla0, s_ret0):
    Bn, T, _ = h.shape
    qa, ka, va, qb, kb, vb, rb, gb, qc, kc, vc, gc = split_columns(h @ w_in)
    heads = lambda t, n: t.reshape(Bn, T, n, t.shape[-1] // n)
    qa, ka, va = heads(qa, H_A), heads(ka, H_A), heads(va, H_A)
    slopes = alibi_slopes()
    if kv_past is None:
        res = [dilated_branch_prompt(qa, ka, va, w, d, slopes) for (w, d) in BRANCHES]
        kv_state = jnp.stack([ka, va], axis=2)[:, T - min(MAX_WINDOW, T):]
    else:
        k_all = jnp.concatenate([kv_past[:, :, 0].astype(ka.dtype), ka], axis=1)
        v_all = jnp.concatenate([kv_past[:, :, 1].astype(va.dtype), va], axis=1)
        res = [dilated_branch_sample(qa, k_all, v_all, w, d, slopes) for (w, d) in BRANCHES]
        kv_state = jnp.stack([ka, va], axis=2)
    o_a = combine_branches([r[0] for r in res], [r[1] for r in res])
    la_b = jax.nn.log_sigmoid((gb @ w_gate2 + b_gate).astype(jnp.float32)) / GATE_TAU
    o_b, s_gla = chunked_gated_linear_attn(heads(qb, H_B) * (DK_B ** -0.5), heads(kb, H_B),
                                           heads(vb, H_B), heads(la_b, H_B), s_gla0)
    o_b = o_b * lax.rsqrt(jnp.mean(o_b * o_b, axis=-1, keepdims=True) + NORM_EPS) * g_gla.astype(jnp.float32)
    o_b = o_b.reshape(Bn, T, V_B) * jax.nn.silu(rb.astype(jnp.float32))
    log_gamma = jnp.log(1.0 - 2.0 ** (-5.0 - jnp.arange(H_C, dtype=jnp.float32)))
    la_c = jnp.broadcast_to(log_gamma[None, None, :, None], (Bn, T, H_C, DK_C))
    o_c, s_ret = chunked_gated_linear_attn(heads(qc, H_C) * (DK_C ** -0.5), heads(kc, H_C),
                                           heads(vc, H_C), la_c, s_ret0)
    mu = jnp.mean(o_c, axis=-1, keepdims=True)
    o_c = (o_c - mu) * lax.rsqrt(jnp.mean(jnp.square(o_c - mu), axis=-1, keepdims=True) + NORM_EPS)
    o_c = o_c.reshape(Bn, T, V_C) * jax.nn.silu(gc.astype(jnp.float32))
    o = jnp.concatenate([o_a.reshape(Bn, T, W_A), o_b, o_c], axis=-1).astype(h.dtype)
    return (o @ w_out).astype(h.dtype), kv_state, s_gla, s_ret


def peer_ffn(h, w_pq, sub_keys, u_tab, v_tab):
    Bn, T, D = h.shape
    xt = h.reshape(-1, D)
    n = xt.shape[0]
    npad = -(-n // PEER_BLOCK) * PEER_BLOCK
    xt = jnp.pad(xt, ((0, npad - n), (0, 0)))
    keys = sub_keys.astype(jnp.float32)

    def block(xb):
        qk = (xb @ w_pq).astype(jnp.float32).reshape(PEER_BLOCK, PEER_HEADS, 2, PEER_DKEY // 2)
        sc = jnp.einsum('thpc,pkc->thpk', qk, keys)
        s1, i1 = lax.top_k(sc[:, :, 0], PEER_TOPK)
        s2, i2 = lax.top_k(sc[:, :, 1], PEER_TOPK)
        cand = (s1[..., :, None] + s2[..., None, :]).reshape(PEER_BLOCK, PEER_HEADS, PEER_TOPK * PEER_TOPK)
        cidx = (i1[..., :, None] * N_KEYS + i2[..., None, :]).reshape(PEER_BLOCK, PEER_HEADS, PEER_TOPK * PEER_TOPK)
        top_s, pos = lax.top_k(cand, PEER_TOPK)
        eidx = jnp.take_along_axis(cidx, pos, axis=-1).reshape(PEER_BLOCK, PEER_HEADS * PEER_TOPK)
        gate = jax.nn.softmax(top_s, axis=-1).reshape(PEER_BLOCK, PEER_HEADS * PEER_TOPK)
        u = jnp.take(u_tab, eidx, axis=0)
        v = jnp.take(v_tab, eidx, axis=0)
        pre = jnp.einsum('td,ted->te', xb, u, preferred_element_type=jnp.float32)
        act = jax.nn.gelu(pre, approximate=False) * gate
        return jnp.einsum('te,ted->td', act, v).astype(h.dtype)

    out = lax.map(block, xt.reshape(-1, PEER_BLOCK, D)).reshape(npad, D)[:n]
    return out.reshape(Bn, T, D)


def trunk(x, kv_win, s_gla, s_ret, w_in, w_gate2, b_gate, g_gla, w_out, g_mix, g_ffn,
          w_pq, sub_keys, u_tab, v_tab, g_final):
    Bn = x.shape[0]
    kv_rows, gla_states, ret_states = [], [], []
    for l in range(DEPTH):
        if kv_win is None:
            kv_l = None
            sg0 = jnp.zeros((Bn, H_B, DK_B, DV_B), jnp.float32)
            sr0 = jnp.zeros((Bn, H_C, DK_C, DV_C), jnp.float32)
        else:
            kv_l, sg0, sr0 = kv_win[l], s_gla[l], s_ret[l]
        y, kv_new, sg, sr = token_mixer(rms_norm(x, g_mix[l]), w_in[l], w_gate2[l], b_gate[l],
                                        g_gla[l], w_out[l], kv_l, sg0, sr0)
        x = x + y
        x = x + peer_ffn(rms_norm(x, g_ffn[l]), w_pq[l], sub_keys[l], u_tab[l], v_tab[l])
        kv_rows.append(kv_new)
        gla_states.append(sg)
        ret_states.append(sr)
    return rms_norm(x, g_final), jnp.stack(kv_rows), jnp.stack(gla_states), jnp.stack(ret_states)


def setup_inputs(seed: int = 0) -> dict:
    key = jax.random.key(seed)
    ks = jax.random.split(key, 18)
    f32 = jnp.float32
    w_buf = min(MAX_WINDOW, PAST_LEN)
    nrm = lambda k, shape, s: jax.random.normal(k, shape, f32) * s
    return {
        "x_prompt": nrm(ks[0], (BATCH, SEQ, D_MODEL), 1.0),
        "x_sample": nrm(ks[1], (DEC_BATCH, DEC_SEQ, D_MODEL), 1.0),
        "cache_kv_win": nrm(ks[2], (DEPTH, DEC_BATCH, w_buf, 2, H_A, HEAD_DIM), 1.0),
        "state_gla": nrm(ks[3], (DEPTH, DEC_BATCH, H_B, DK_B, DV_B), 1.0),
        "state_ret": nrm(ks[4], (DEPTH, DEC_BATCH, H_C, DK_C, DV_C), 1.0),
        "w_in": nrm(ks[5], (DEPTH, D_MODEL, D_IN), D_MODEL ** -0.5),
        "w_gate2": nrm(ks[6], (DEPTH, GATE_RANK, QK_B), GATE_RANK ** -0.5),
        "b_gate": nrm(ks[7], (DEPTH, QK_B), 0.1),
        "g_gla": 1.0 + nrm(ks[8], (DEPTH, DV_B), 0.01),
        "w_out": nrm(ks[9], (DEPTH, MIX_WIDTH, D_MODEL), MIX_WIDTH ** -0.5),
        "g_mix": 1.0 + nrm(ks[10], (DEPTH, D_MODEL), 0.01),
        "g_ffn": 1.0 + nrm(ks[11], (DEPTH, D_MODEL), 0.01),
        "w_pq": nrm(ks[12], (DEPTH, D_MODEL, PEER_HEADS * PEER_DKEY), D_MODEL ** -0.5),
        "sub_keys": nrm(ks[13], (DEPTH, 2, N_KEYS, PEER_DKEY // 2), (PEER_DKEY // 2) ** -0.5),
        "u_tab": nrm(ks[14], (DEPTH, N_EXPERTS, D_MODEL), D_MODEL ** -0.5),
        "v_tab": nrm(ks[15], (DEPTH, N_EXPERTS, D_MODEL), (PEER_HEADS * PEER_TOPK) ** -0.5),
        "g_final": 1.0 + nrm(ks[16], (D_MODEL,), 0.01),
    }


def reference(x_prompt, x_sample, cache_kv_win, state_gla, state_ret, w_in, w_gate2, b_gate, g_gla,
              w_out, g_mix, g_ffn, w_pq, sub_keys, u_tab, v_tab, g_final):
    y_prompt, kv_win_prompt, gla_prompt, ret_prompt = trunk(
        x_prompt, None, None, None, w_in, w_gate2, b_gate, g_gla, w_out, g_mix, g_ffn,
        w_pq, sub_keys, u_tab, v_tab, g_final)
    y_sample, kv_win_sample, gla_sample, ret_sample = trunk(
        x_sample, cache_kv_win, state_gla, state_ret, w_in, w_gate2, b_gate, g_gla, w_out, g_mix, g_ffn,
        w_pq, sub_keys, u_tab, v_tab, g_final)
    return (y_prompt, y_sample, kv_win_prompt, kv_win_sample, gla_prompt, gla_sample, ret_prompt, ret_sample)
```

```python
import math
from contextlib import ExitStack
import numpy as np
import concourse.bass as bass
import concourse.mybir as mybir
from concourse.bass_utils import run_bass_kernel_spmd

F32 = mybir.dt.float32
BF16 = mybir.dt.bfloat16
I32 = mybir.dt.int32
U32 = mybir.dt.uint32
AF = mybir.ActivationFunctionType
ALU = mybir.AluOpType
AX = mybir.AxisListType

NCORES = 8
DM = 1024
DIN = 3088
NEG = -30000.0
EPS = 1e-6
SLOPES = [2.0 ** (-8.0 * (i + 1) / 8) for i in range(8)]
DILS = (1, 4, 16)
NSQ = 16
TS = 64

EPOCH = 24000
DMA_EPOCH = 1500


class Res:
    __slots__ = ("name", "w", "r")

    def __init__(self, name=""):
        self.name = name
        self.w = None
        self.r = []


class _Rec:
    def __init__(self):
        self.call = None

    def __getattr__(self, name):
        def f(*a, **k):
            self.call = (name, a, k)
            return self
        return f


def _record(fn):
    rec = _Rec()
    fn(rec)
    assert rec.call is not None
    return rec.call


class Sched:
    ENG = ("pe", "act", "dve", "pool", "sp")

    def __init__(self, nc):
        self.nc = nc
        self.lists = {e: [] for e in self.ENG}
        self.tick = {}
        self.sems = {}
        self.seen = {e: {} for e in self.ENG}
        self.dma_slots = {"sp": 8, "act": 4, "pool": 8}
        self.dma_n = {q: 0 for q in self.dma_slots}
        self.ninstr = 0
        self._semctx = []

    def _sem(self, key, epoch):
        k = (key, epoch)
        if k not in self.sems:
            name = "s%d" % len(self.sems)
            cm = self.nc.semaphore(name)
            self._semctx.append(cm)
            self.sems[k] = cm.__enter__()
        return self.sems[k]

    def _locate(self, key, tick):
        if isinstance(key, tuple):
            ep = (tick - 1) // DMA_EPOCH
            return self._sem(key, ep), 16 * (tick - ep * DMA_EPOCH)
        ep = (tick - 1) // EPOCH
        return self._sem(key, ep), tick - ep * EPOCH

    def _wait(self, eng, key, tick):
        if self.seen[eng].get(key, 0) >= tick:
            return
        self.seen[eng][key] = tick
        sem, val = self._locate(key, tick)
        self.lists[eng].append(("wait", sem, val))

    def _deps(self, eng, reads, writes):
        need = {}
        for r in reads:
            if r.w is not None:
                k, t = r.w
                if need.get(k, 0) < t:
                    need[k] = t
        for w in writes:
            if w.w is not None:
                k, t = w.w
                if need.get(k, 0) < t:
                    need[k] = t
            for (k, t) in w.r:
                if need.get(k, 0) < t:
                    need[k] = t
        for k, t in need.items():
            self._wait(eng, k, t)

    def _mark(self, key, tick, reads, writes):
        for r in reads:
            r.r.append((key, tick))
            if len(r.r) > 48:
                d = {}
                for (k, t) in r.r:
                    if d.get(k, 0) < t:
                        d[k] = t
                r.r = list(d.items())
        for w in writes:
            w.w = (key, tick)
            w.r = []

    def op(self, eng, fn, reads=(), writes=()):
        self._deps(eng, reads, writes)
        t = self.tick.get(eng, 0) + 1
        self.tick[eng] = t
        sem, _ = self._locate(eng, t)
        self.lists[eng].append(("op", _record(fn), sem, 1))
        self._mark(eng, t, reads, writes)
        self.ninstr += 1

    def dma(self, q, fn, reads=(), writes=()):
        n = self.dma_n[q]
        self.dma_n[q] = n + 1
        ns = self.dma_slots[q]
        slot = n % ns
        key = ("d", q, slot)
        t = n // ns + 1
        if t > 1:
            self._wait(q, key, t - 1)
        self._deps(q, reads, writes)
        sem, _ = self._locate(key, t)
        self.lists[q].append(("op", _record(fn), sem, 16))
        self.tick[key] = t
        self._mark(key, t, reads, writes)
        self.ninstr += 1

    def wait_all(self, eng):
        for key, t in list(self.tick.items()):
            if t > 0:
                self._wait(eng, key, t)

    def barrier(self):
        for e in self.ENG:
            self.wait_all(e)

    def emit(self):
        nc = self.nc
        lists = self.lists

        def play(engobj, items):
            for it in items:
                if it[0] == "wait":
                    engobj.wait_ge(it[1], it[2])
                else:
                    name, a, k = it[1]
                    ins = getattr(engobj, name)(*a, **k)
                    ins.then_inc(it[2], it[3])

        with nc.Block() as block:
            @block.sync
            def _(e):
                play(e, lists["sp"])

            @block.scalar
            def _(e):
                play(e, lists["act"])

            @block.vector
            def _(e):
                play(e, lists["dve"])

            @block.gpsimd
            def _(e):
                play(e, lists["pool"])

            @block.tensor
            def _(e):
                play(e, lists["pe"])

    def close(self):
        for cm in reversed(self._semctx):
            cm.__exit__(None, None, None)
        self._semctx = []


class T:
    __slots__ = ("a", "r")

    def __init__(self, a, name=""):
        self.a = a
        self.r = Res(name)


C_ID = 0
C_TRI = 128
C_ONE = 256
C_SCUM = 384
C_SSAME = 512
C_LGAM = 640
C_SEQI = 768
C_HM = 784
C_HALF = 788
C_ONEC = 790
C_IOTA = 792
NCF = 792 + 256


def make_consts():
    c = np.zeros((128, NCF), np.float32)
    p = np.arange(128)
    c[:, C_ID:C_ID + 128] = np.eye(128, dtype=np.float32)
    c[:, C_TRI:C_TRI + 128] = (p[:, None] <= p[None, :]).astype(np.float32)
    c[:, C_ONE:C_ONE + 128] = 1.0
    same = (p[:, None] // 4 == p[None, :] // 4) & (p[:, None] < TS) & (p[None, :] < TS)
    c[:, C_SCUM:C_SCUM + 128] = (same & (p[:, None] <= p[None, :])).astype(np.float32)
    c[:, C_SSAME:C_SSAME + 128] = same.astype(np.float32)
    lg = np.log(1.0 - 2.0 ** (-5.0 - np.arange(4, dtype=np.float64))).astype(np.float32)
    c[:, C_LGAM:C_LGAM + 128] = np.repeat(lg, 32)[None, :]
    c[:, C_SEQI:C_SEQI + 16] = ((p[:, None] // 4 == np.arange(16)[None, :]) & (p[:, None] < TS)).astype(np.float32)
    c[:, C_HM:C_HM + 4] = (p[:, None] // 32 == np.arange(4)[None, :]).astype(np.float32)
    c[:, C_HALF:C_HALF + 2] = (p[:, None] // 64 == np.arange(2)[None, :]).astype(np.float32)
    c[:, C_ONEC] = 1.0
    c[:, C_IOTA:C_IOTA + 256] = np.arange(256, dtype=np.float32)[None, :]
    bias = np.zeros((128, 3, 2, 8, 128), np.float32)
    k = np.arange(128)[:, None]
    q = np.arange(128)[None, :]
    for g, d in enumerate(DILS):
        for h in range(8):
            dist = q + 128 - k
            b = np.where(k >= q, -SLOPES[h] * dist * d, NEG)
            bias[:, g, 0, h, :] = b
            dist = q - k
            b = np.where(k <= q, -SLOPES[h] * dist * d, NEG)
            bias[:, g, 1, h, :] = b
    bias = bias.reshape(128, 48 * 128)
    sb = np.full((128, 9, 8, 4), NEG, np.float32)
    i = np.arange(128)
    for h in range(8):
        for t in range(4):
            dist = 128 + t - i
            sb[:, 0, h, t] = np.where(i >= t, -SLOPES[h] * dist, NEG)
            sb[:, 1 + t, h, t] = -SLOPES[h] * (512 - 4 * i)
            sb[:, 5 + t, h, t] = -SLOPES[h] * (2048 - 16 * i)
    sb = sb.reshape(128, 288)
    nb = np.full((128, 2, 8, 64), NEG, np.float32)
    kt = np.arange(64)[:, None]
    qt = np.arange(64)[None, :]
    sameq = (kt // 4 == qt // 4)
    for h in range(8):
        nb[:64, 0, h, :] = np.where(sameq & (kt <= qt), -SLOPES[h] * (qt - kt), NEG)
        nb[:64, 1, h, :] = np.where(kt == qt, 0.0, NEG)
    nb = nb.reshape(128, 1024)
    seqcol = np.zeros((128, 16, 64), np.float32)
    for s in range(16):
        seqcol[:, s, 4 * s:4 * s + 4] = 1.0
    seqcol = seqcol.reshape(128, 1024)
    return c, bias, sb, nb, seqcol


def build(NB):
    NT = NB * 128
    NR = NT + 128
    KVB0 = NB - 16
    nc = bass.Bass("TRN2", target_bir_lowering=False)
    dt_in = lambda n, s, d=F32: nc.dram_tensor(n, list(s), d, kind="ExternalInput")
    dt_out = lambda n, s, d=F32: nc.dram_tensor(n, list(s), d, kind="ExternalOutput")
    xp = dt_in("xp", [NT, DM])
    xs = dt_in("xs", [128, DM])
    cache = dt_in("cache", [2, NSQ, 2048, 1024])
    sgla = dt_in("sgla", [2, NSQ, 128, 64])
    sret = dt_in("sret", [2, NSQ, 128, 64])
    w_in = dt_in("w_in", [2, DM, DIN])
    w_g2 = dt_in("w_g2", [2, 16, 128])
    b_g = dt_in("b_g", [2, 1, 128])
    g_gla = dt_in("g_gla", [2, 64])
    w_out = dt_in("w_out", [2, DM, DM])
    g_mix = dt_in("g_mix", [2, DM])
    g_ffn = dt_in("g_ffn", [2, DM])
    w_pq = dt_in("w_pq", [2, DM, 2048])
    keysT = dt_in("keysT", [2, 2, 128, 128])
    uvs = [dt_in("uv0", [16384, 2048]), dt_in("uv1", [16384, 2048])]
    g_fin = dt_in("g_fin", [DM])
    c_f32 = dt_in("c_f32", [128, NCF])
    c_bias = dt_in("c_bias", [128, 48 * 128])
    c_sb = dt_in("c_sb", [128, 288])
    c_nb = dt_in("c_nb", [128, 1024])
    c_sc = dt_in("c_sc", [128, 1024])

    y_p = dt_out("y_p", [NT, DM])
    y_s = dt_out("y_s", [TS, DM])
    kvp = dt_out("kvp", [2, 2048, 1024])
    kvs = dt_out("kvs", [2, TS, 1024])
    glap = dt_out("glap", [2, 128, 64])
    glas = dt_out("glas", [2, NSQ, 128, 64])
    retp = dt_out("retp", [2, 128, 64])
    rets = dt_out("rets", [2, NSQ, 128, 64])

    Z = nc.dram_tensor("Zs", [NR, 1536], F32)
    OBC = nc.dram_tensor("OBCs", [NR, 512], F32)
    OA = nc.dram_tensor("OAs", [3, NT, 520], F32)
    OAS = nc.dram_tensor("OASs", [128, 512], F32)
    XR = nc.dram_tensor("XRs", [NR, DM], F32)

    S = Sched(nc)
    es = ExitStack()

    def sb(name, shape, dt=F32):
        return es.enter_context(nc.sbuf_tensor(name, list(shape), dt))

    def V(fn, r=(), w=()):
        S.op("dve", fn, [t.r for t in r], [t.r for t in w])

    def A(fn, r=(), w=()):
        S.op("act", fn, [t.r for t in r], [t.r for t in w])

    def G(fn, r=(), w=()):
        S.op("pool", fn, [t.r for t in r], [t.r for t in w])

    def P(fn, r=(), w=()):
        S.op("pe", fn, [t.r for t in r], [t.r for t in w])

    def D(q, out, in_, r=(), w=()):
        S.dma(q, lambda e: e.dma_start(out=out, in_=in_), [t.r for t in r], [t.r for t in w])

    def mm(out, lhsT, rhs, start, stop, r, w):
        P(lambda e: e.matmul(out, lhsT=lhsT, rhs=rhs, start=start, stop=stop), r, w)

    def tr(out, in_, ident, r, w):
        P(lambda e: e.transpose(out=out, in_=in_, identity=ident), r, w)

    wbig = T(sb("wbig", [128, 8 * DIN], BF16), "wbig")
    z = T(sb("z", [128, DIN], F32), "z")
    cf = T(sb("cf", [128, NCF], F32), "cf")
    identb = T(sb("identb", [128, 128], BF16), "identb")
    onesb = T(sb("onesb", [1, 128], BF16), "onesb")
    Sst = [T(sb("Sg", [128, 64], F32), "Sg"), T(sb("Sr", [128, 64], F32), "Sr")]
    arena_f = sb("arena_f", [128, 24576], F32)
    arena_b = sb("arena_b", [128, 16896], BF16)
    pf = [T(es.enter_context(nc.psum_tensor("pf%d" % i, [128, 512], F32)), "pf%d" % i) for i in range(6)]
    pb = [T(es.enter_context(nc.psum_tensor("pb%d" % i, [128, 1024], BF16)), "pb%d" % i) for i in range(2)]
    cnt = {"f": 0, "b": 0, "af": 0, "ab": 0, "ev": 0, "dq": 0}

    def nf():
        cnt["f"] += 1
        return pf[cnt["f"] % 6]

    def nbk():
        cnt["b"] += 1
        return pb[cnt["b"] % 2]

    def reset_arena():
        cnt["af"] = 0
        cnt["ab"] = 0

    def cf32(n, name=""):
        o = cnt["af"]
        cnt["af"] = o + n
        assert cnt["af"] <= 24576, ("arena_f overflow", name, cnt["af"])
        return T(arena_f[:, o:o + n], name)

    def cb16(n, name=""):
        o = cnt["ab"]
        n2 = (n + 1) // 2 * 2
        cnt["ab"] = o + n2
        assert cnt["ab"] <= 16896, ("arena_b overflow", name, cnt["ab"])
        return T(arena_b[:, o:o + n], name)

    def evac(out, in_, r, w):
        cnt["ev"] += 1
        if cnt["ev"] % 2:
            A(lambda e: e.activation(out=out, in_=in_, func=AF.Copy), r, w)
        else:
            V(lambda e: e.tensor_copy(out=out, in_=in_), r, w)

    dummy = lambda: T(None, "dummy")

    D("sp", cf.a[:], c_f32.ap(), w=[cf])
    V(lambda e: e.tensor_copy(out=identb.a[:], in_=cf.a[:, C_ID:C_ID + 128]), [cf], [identb])
    V(lambda e: e.tensor_copy(out=onesb.a[:], in_=cf.a[0:1, C_ONE:C_ONE + 128]), [cf], [onesb])
    identf = cf.a[:, C_ID:C_ID + 128]
    HM = cf.a[:, C_HM:C_HM + 4]
    HALF = cf.a[:, C_HALF:C_HALF + 2]

    def load_weight(dram3, l, ncols, dst_view):
        for kc in range(8):
            D("sp", z.a[:, 0:ncols], dram3.ap()[l, kc * 128:(kc + 1) * 128, :], w=[z])
            G(lambda e, kc=kc: e.tensor_copy(out=dst_view[:, kc, :], in_=z.a[:, 0:ncols]), [z], [wbig])

    def rmsnorm(x_t, g_t, outs, junk, ss):
        A(lambda e: e.activation(out=junk.a[:], in_=x_t.a[:], func=AF.Square, accum_out=ss.a[:, 0:1]), [x_t], [junk, ss])
        V(lambda e: e.tensor_scalar(out=ss.a[:], in0=ss.a[:], scalar1=1.0 / DM, scalar2=EPS, op0=ALU.mult, op1=ALU.add), [ss], [ss])
        A(lambda e: e.activation(out=ss.a[:], in_=ss.a[:], func=AF.Sqrt), [ss], [ss])
        V(lambda e: e.reciprocal(out=ss.a[:], in_=ss.a[:]), [ss], [ss])
        for (t, ap) in outs:
            V(lambda e, ap=ap: e.scalar_tensor_tensor(out=ap, in0=x_t.a[:], scalar=ss.a[:, 0:1], in1=g_t.a[:],
                                                       op0=ALU.mult, op1=ALU.mult), [x_t, ss, g_t], [t])

    def transpose8(src_b, dstT, n=8):
        p = nbk()
        for kc in range(n):
            tr(p.a[:, kc * 128:(kc + 1) * 128], src_b.a[:, kc * 128:(kc + 1) * 128], identb.a[:], [src_b, identb], [p])
        evac(dstT.a[:, 0:n * 128], p.a[:, 0:n * 128], [p], [dstT])

    for l in range(2):
        S.barrier()
        reset_arena()
        w_in_v = wbig.a[:, :].rearrange("p (k c) -> p k c", k=8)
        load_weight(w_in, l, DIN, w_in_v)
        gmix = cf32(1024, "gmix")
        D("sp", gmix.a[:], g_mix.ap()[l].partition_broadcast(128), w=[gmix])
        ggla = cf32(64, "ggla")
        D("sp", ggla.a[:], g_gla.ap()[l].partition_broadcast(128), w=[ggla])
        wg2f = cf32(128, "wg2f")
        D("sp", wg2f.a[0:16, :], w_g2.ap()[l], w=[wg2f])
        bgf = cf32(128, "bgf")
        D("sp", bgf.a[0:1, :], b_g.ap()[l], w=[bgf])
        wg2 = cb16(128, "wg2")
        bgb = cb16(128, "bgb")
        V(lambda e: e.tensor_copy(out=wg2.a[0:16, :], in_=wg2f.a[0:16, :]), [wg2f], [wg2])
        V(lambda e: e.tensor_copy(out=bgb.a[0:1, :], in_=bgf.a[0:1, :]), [bgf], [bgb])
        S0 = [cf32(1024, "S0g"), cf32(1024, "S0r")]
        D("sp", S0[0].a[:, :].rearrange("p (s e) -> p s e", s=NSQ), sgla.ap()[l].rearrange("s p e -> p s e"), w=[S0[0]])
        D("sp", S0[1].a[:, :].rearrange("p (s e) -> p s e", s=NSQ), sret.ap()[l].rearrange("s p e -> p s e"), w=[S0[1]])
        Sfin = [cf32(1024, "Sfg"), cf32(1024, "Sfr")]
        seqcol = cb16(1024, "seqcol")
        scst = cf32(1024, "scst")
        D("sp", scst.a[:], c_sc.ap(), w=[scst])
        V(lambda e: e.tensor_copy(out=seqcol.a[:], in_=scst.a[:]), [scst], [seqcol])
        for m in range(2):
            V(lambda e, m=m: e.memset(Sst[m].a[:], 0.0), [], [Sst[m]])
        xt = cf32(1024, "x")
        junkA = cf32(1024, "junkA")
        ssA = cf32(1, "ssA")
        hnb = cb16(1024, "hnb")
        hnT = cb16(1024, "hnT")
        la = cf32(256, "la")
        V(lambda e: e.tensor_copy(out=la.a[:, 128:256], in_=cf.a[:, C_LGAM:C_LGAM + 128]), [cf], [la])
        gbb = cb16(16, "gbb")
        gbT = cb16(128, "gbT")
        e1 = cf32(128, "e1")
        bs = cf32(256, "bs")
        eb = cf32(256, "eb")
        enb = cf32(256, "enb")
        ebl = cf32(256, "ebl")
        dec = cf32(32, "dec")
        qtb = cb16(128, "qtb")
        ktb = cb16(128, "ktb")
        khb = cb16(128, "khb")
        khm = cb16(128, "khm")
        vb16 = cb16(256, "vb16")
        qkT = cb16(256, "qkT")
        qTm = cb16(512, "qTm")
        qTcm = cb16(1024, "qTcm")
        attm = cb16(512, "attm")
        Sbd = [cb16(256, "Sbdg"), cb16(256, "Sbdr")]
        Sbds = [cb16(NSQ * 256, "Sbdsg"), cb16(NSQ * 256, "Sbdsr")]
        o1 = cf32(256, "o1")
        om = cf32(256, "om")
        tmp = cf32(256, "tmp")
        kvsel = cf32(64, "kvsel")
        st4 = cf32(8, "st4")
        sil = cf32(256, "sil")
        obc = cf32(512, "obc")
        for m in range(2):
            V(lambda e, m=m: e.tensor_tensor(
                out=Sbds[m].a[:, :].rearrange("p (s h e) -> p s h e", s=NSQ, h=4),
                in0=S0[m].a[:, :].rearrange("p (s e) -> p s e", s=NSQ).unsqueeze(2).to_broadcast([128, NSQ, 4, 64]),
                in1=HM.unsqueeze(1).unsqueeze(3).to_broadcast([128, NSQ, 4, 64]), op=ALU.mult), [S0[m], cf], [Sbds[m]])

        for b in range(NB + 1):
            samp = (b == NB)
            Tn = TS if samp else 128
            nseq = NSQ if samp else 1
            if l == 0:
                src = xs.ap() if samp else xp.ap()[b * 128:(b + 1) * 128, :]
            else:
                src = XR.ap()[b * 128:(b + 1) * 128, :]
            D("sp", xt.a[:], src, w=[xt])
            rmsnorm(xt, gmix, [(hnb, hnb.a[:])], junkA, ssA)
            transpose8(hnb, hnT)
            hnTv = hnT.a[:, :].rearrange("p (k t) -> p k t", k=8)
            for n in range(7):
                c0 = n * 512
                cw = min(512, DIN - c0)
                p = nf()
                for kc in range(8):
                    mm(p.a[:, 0:cw], hnTv[:, kc, :], w_in_v[:, kc, c0:c0 + cw], kc == 0, kc == 7, [hnT, wbig], [p])
                evac(z.a[:, c0:c0 + cw], p.a[:, 0:cw], [p], [z])
            D("sp", Z.ap()[b * 128:(b + 1) * 128, :], z.a[:, 0:1536], r=[z], w=[dummy()])
            if samp:
                D("sp", kvs.ap()[l], z.a[0:TS, 512:1536], r=[z], w=[dummy()])
            elif b >= KVB0:
                D("sp", kvp.ap()[l, (b - KVB0) * 128:(b - KVB0 + 1) * 128, :], z.a[:, 512:1536], r=[z], w=[dummy()])

            CUM = cf.a[:, C_SCUM:C_SCUM + 128] if samp else cf.a[:, C_TRI:C_TRI + 128]
            TOT = cf.a[:, C_SSAME:C_SSAME + 128] if samp else cf.a[:, C_ONE:C_ONE + 128]
            SEQI = cf.a[:, C_SEQI:C_SEQI + 16] if samp else cf.a[:, C_ONEC:C_ONEC + 1]
            V(lambda e: e.tensor_copy(out=gbb.a[:Tn, :], in_=z.a[:Tn, 2304:2320]), [z], [gbb])
            p = nbk()
            tr(p.a[0:16, 0:Tn], gbb.a[:Tn, 0:16], identb.a[:Tn, :Tn], [gbb, identb], [p])
            evac(gbT.a[0:16, 0:Tn], p.a[0:16, 0:Tn], [p], [gbT])
            pg = nf()
            mm(pg.a[:Tn, 0:128], gbT.a[0:16, 0:Tn], wg2.a[0:16, :], True, False, [gbT, wg2], [pg])
            mm(pg.a[:Tn, 0:128], onesb.a[0:1, 0:Tn], bgb.a[0:1, :], False, True, [onesb, bgb], [pg])
            A(lambda e, pg=pg: e.activation(out=e1.a[:Tn, :], in_=pg.a[:Tn, 0:128], func=AF.Exp, scale=-1.0), [pg], [e1])
            A(lambda e: e.activation(out=e1.a[:Tn, :], in_=e1.a[:Tn, :], func=AF.Ln, bias=1.0), [e1], [e1])
            A(lambda e: e.activation(out=la.a[:Tn, 0:128], in_=e1.a[:Tn, :], func=AF.Copy, scale=-1.0 / 16.0), [e1], [la])
            pc = nf()
            mm(pc.a[:Tn, 0:256], CUM[:Tn, :Tn], la.a[:Tn, :], True, True, [cf, la], [pc])
            mm(pc.a[:Tn, 256:512], TOT[:Tn, :Tn], la.a[:Tn, :], True, True, [cf, la], [pc])
            pd = nf()
            mm(pd.a[:, 0:nseq], la.a[:Tn, 0:128], SEQI[:Tn, 0:nseq], True, True, [la, cf], [pd])
            mm(pd.a[:, 16:16 + nseq], la.a[:Tn, 128:256], SEQI[:Tn, 0:nseq], True, True, [la, cf], [pd])
            A(lambda e, pd=pd: e.activation(out=dec.a[:, :], in_=pd.a[:, 0:32], func=AF.Exp), [pd], [dec])
            A(lambda e, pc=pc: e.activation(out=bs.a[:Tn, :], in_=pc.a[:Tn, 0:256], func=AF.Copy), [pc], [bs])
            A(lambda e: e.activation(out=eb.a[:Tn, :], in_=bs.a[:Tn, :], func=AF.Exp), [bs], [eb])
            A(lambda e: e.activation(out=enb.a[:Tn, :], in_=bs.a[:Tn, :], func=AF.Exp, scale=-1.0), [bs], [enb])
            V(lambda e, pc=pc: e.tensor_tensor(out=ebl.a[:Tn, :], in0=pc.a[:Tn, 256:512], in1=bs.a[:Tn, :], op=ALU.subtract), [pc, bs], [ebl])
            A(lambda e: e.activation(out=ebl.a[:Tn, :], in_=ebl.a[:Tn, :], func=AF.Exp), [ebl], [ebl])
            for m in range(2):
                qc0 = 1536 if m == 0 else 2320
                kc0 = qc0 + 128
                vc0 = qc0 + 256
                gc0 = qc0 + 512 if m == 0 else qc0 + 512
                gate0 = 2048 if m == 0 else 2832
                ms = slice(m * 128, (m + 1) * 128)
                V(lambda e, qc0=qc0, ms=ms: e.scalar_tensor_tensor(out=qtb.a[:Tn, :], in0=z.a[:Tn, qc0:qc0 + 128], scalar=32.0 ** -0.5,
                                                                    in1=eb.a[:Tn, ms], op0=ALU.mult, op1=ALU.mult), [z, eb], [qtb])
                V(lambda e, kc0=kc0, ms=ms: e.tensor_tensor(out=ktb.a[:Tn, :], in0=z.a[:Tn, kc0:kc0 + 128], in1=enb.a[:Tn, ms], op=ALU.mult), [z, enb], [ktb])
                V(lambda e, kc0=kc0, ms=ms: e.tensor_tensor(out=khb.a[:Tn, :], in0=z.a[:Tn, kc0:kc0 + 128], in1=ebl.a[:Tn, ms], op=ALU.mult), [z, ebl], [khb])
                G(lambda e, vc0=vc0: e.tensor_copy(out=vb16.a[:Tn, :], in_=z.a[:Tn, vc0:vc0 + 256]), [z], [vb16])
                p = nbk()
                tr(p.a[:, 0:Tn], qtb.a[:Tn, :], identb.a[:Tn, :Tn], [qtb, identb], [p])
                tr(p.a[:, 128:128 + Tn], ktb.a[:Tn, :], identb.a[:Tn, :Tn], [ktb, identb], [p])
                evac(qkT.a[:, 0:256], p.a[:, 0:256], [p], [qkT])
                V(lambda e: e.tensor_tensor(out=qTm.a[:, :].rearrange("p (h t) -> p h t", h=4)[:, :, 0:Tn],
                                            in0=qkT.a[:, 0:Tn].unsqueeze(1).to_broadcast([128, 4, Tn]),
                                            in1=HM.unsqueeze(2).to_broadcast([128, 4, Tn]), op=ALU.mult), [qkT, cf], [qTm])
                pa = nf()
                for h in range(4):
                    mm(pa.a[:Tn, h * 128:h * 128 + Tn], qkT.a[:, 128:128 + Tn], qTm.a[:, h * 128:h * 128 + Tn], True, True, [qkT, qTm], [pa])
                V(lambda e, pa=pa: e.tensor_tensor(out=attm.a[:Tn, :].rearrange("p (h t) -> p h t", h=4)[:, :, 0:Tn],
                                                   in0=pa.a[:Tn, :].rearrange("p (h t) -> p h t", h=4)[:, :, 0:Tn],
                                                   in1=CUM[:Tn, :Tn].unsqueeze(1).to_broadcast([Tn, 4, Tn]), op=ALU.mult), [pa, cf], [attm])
                po = nf()
                for h in range(4):
                    mm(po.a[:Tn, h * 64:(h + 1) * 64], attm.a[:Tn, h * 128:h * 128 + Tn], vb16.a[:Tn, h * 64:(h + 1) * 64], True, True, [attm, vb16], [po])
                if not samp:
                    V(lambda e, m=m: e.tensor_tensor(out=Sbd[m].a[:, :].rearrange("p (h e) -> p h e", h=4),
                                                     in0=Sst[m].a[:, :].unsqueeze(1).to_broadcast([128, 4, 64]),
                                                     in1=HM.unsqueeze(2).to_broadcast([128, 4, 64]), op=ALU.mult), [Sst[m], cf], [Sbd[m]])
                    mm(po.a[:Tn, 256:512], qkT.a[:, 0:Tn], Sbd[m].a[:, :], True, True, [qkT, Sbd[m]], [po])
                else:
                    V(lambda e: e.tensor_tensor(out=qTcm.a[:, :].rearrange("p (s t) -> p s t", s=NSQ),
                                                in0=qkT.a[:, 0:TS].unsqueeze(1).to_broadcast([128, NSQ, TS]),
                                                in1=seqcol.a[:, :].rearrange("p (s t) -> p s t", s=NSQ), op=ALU.mult), [qkT, seqcol], [qTcm])
                    for s in range(NSQ):
                        mm(po.a[:Tn, 256:512], qTcm.a[:, s * TS:(s + 1) * TS], Sbds[m].a[:, s * 256:(s + 1) * 256], s == 0, s == NSQ - 1, [qTcm, Sbds[m]], [po])
                A(lambda e, po=po: e.activation(out=o1.a[:Tn, :], in_=po.a[:Tn, 0:256], func=AF.Copy), [po], [o1])
                V(lambda e, po=po: e.tensor_tensor(out=om.a[:Tn, :], in0=po.a[:Tn, 256:512], in1=o1.a[:Tn, :], op=ALU.add), [po, o1], [om])
                if not samp:
                    pk = nf()
                    mm(pk.a[:, 0:256], khb.a[:Tn, :], vb16.a[:Tn, :], True, True, [khb, vb16], [pk])
                    V(lambda e, pk=pk: e.tensor_tensor(out=tmp.a[:, :].rearrange("p (h e) -> p h e", h=4),
                                                       in0=pk.a[:, 0:256].rearrange("p (h e) -> p h e", h=4),
                                                       in1=HM.unsqueeze(2).to_broadcast([128, 4, 64]), op=ALU.mult), [pk, cf], [tmp])
                    V(lambda e: e.tensor_reduce(out=kvsel.a[:, :], in_=tmp.a[:, :].rearrange("p (h e) -> p e h", h=4), axis=AX.X, op=ALU.add), [tmp], [kvsel])
                    V(lambda e, m=m: e.scalar_tensor_tensor(out=Sst[m].a[:, :], in0=Sst[m].a[:, :], scalar=dec.a[:, m * 16:m * 16 + 1],
                                                            in1=kvsel.a[:, :], op0=ALU.mult, op1=ALU.add), [Sst[m], dec, kvsel], [Sst[m]])
                    if b == NB - 1:
                        D("sp", (glap if m == 0 else retp).ap()[l], Sst[m].a[:, :], r=[Sst[m]], w=[dummy()])
                else:
                    for s in range(NSQ):
                        V(lambda e, s=s: e.tensor_scalar(out=khm.a[:Tn, :], in0=khb.a[:Tn, :], scalar1=SEQI[:Tn, s:s + 1], scalar2=None, op0=ALU.mult), [khb, cf], [khm])
                        pk = nf()
                        mm(pk.a[:, 0:256], khm.a[:Tn, :], vb16.a[:Tn, :], True, True, [khm, vb16], [pk])
                        V(lambda e, pk=pk: e.tensor_tensor(out=tmp.a[:, :].rearrange("p (h e) -> p h e", h=4),
                                                           in0=pk.a[:, 0:256].rearrange("p (h e) -> p h e", h=4),
                                                           in1=HM.unsqueeze(2).to_broadcast([128, 4, 64]), op=ALU.mult), [pk, cf], [tmp])
                        V(lambda e: e.tensor_reduce(out=kvsel.a[:, :], in_=tmp.a[:, :].rearrange("p (h e) -> p e h", h=4), axis=AX.X, op=ALU.add), [tmp], [kvsel])
                        V(lambda e, m=m, s=s: e.scalar_tensor_tensor(out=Sfin[m].a[:, s * 64:(s + 1) * 64], in0=S0[m].a[:, s * 64:(s + 1) * 64],
                                                                    scalar=dec.a[:, m * 16 + s:m * 16 + s + 1], in1=kvsel.a[:, :],
                                                                    op0=ALU.mult, op1=ALU.add), [S0[m], dec, kvsel], [Sfin[m]])
                    D("sp", (glas if m == 0 else rets).ap()[l].rearrange("s p e -> p s e"),
                      Sfin[m].a[:, :].rearrange("p (s e) -> p s e", s=NSQ), r=[Sfin[m]], w=[dummy()])
                om3 = om.a[:Tn, :].rearrange("p (h e) -> p h e", h=4)
                tmp3 = tmp.a[:Tn, :].rearrange("p (h e) -> p h e", h=4)
                if m == 1:
                    V(lambda e: e.tensor_reduce(out=st4.a[:Tn, 0:4], in_=om3, axis=AX.X, op=ALU.add), [om], [st4])
                    V(lambda e: e.tensor_scalar(out=st4.a[:Tn, 0:4], in0=st4.a[:Tn, 0:4], scalar1=-1.0 / 64.0, scalar2=None, op0=ALU.mult), [st4], [st4])
                    V(lambda e: e.tensor_tensor(out=om3, in0=om3, in1=st4.a[:Tn, 0:4].unsqueeze(2).to_broadcast([Tn, 4, 64]), op=ALU.add), [om, st4], [om])
                V(lambda e: e.tensor_tensor(out=tmp.a[:Tn, :], in0=om.a[:Tn, :], in1=om.a[:Tn, :], op=ALU.mult), [om], [tmp])
                V(lambda e: e.tensor_reduce(out=st4.a[:Tn, 4:8], in_=tmp3, axis=AX.X, op=ALU.add), [tmp], [st4])
                V(lambda e: e.tensor_scalar(out=st4.a[:Tn, 4:8], in0=st4.a[:Tn, 4:8], scalar1=1.0 / 64.0, scalar2=EPS, op0=ALU.mult, op1=ALU.add), [st4], [st4])
                A(lambda e: e.activation(out=st4.a[:Tn, 4:8], in_=st4.a[:Tn, 4:8], func=AF.Sqrt), [st4], [st4])
                V(lambda e: e.reciprocal(out=st4.a[:Tn, 4:8], in_=st4.a[:Tn, 4:8]), [st4], [st4])
                V(lambda e: e.tensor_tensor(out=om3, in0=om3, in1=st4.a[:Tn, 4:8].unsqueeze(2).to_broadcast([Tn, 4, 64]), op=ALU.mult), [om, st4], [om])
                if m == 0:
                    V(lambda e: e.tensor_tensor(out=om3, in0=om3, in1=ggla.a[:Tn, :].unsqueeze(1).to_broadcast([Tn, 4, 64]), op=ALU.mult), [om, ggla], [om])
                A(lambda e, gate0=gate0: e.activation(out=sil.a[:Tn, :], in_=z.a[:Tn, gate0:gate0 + 256], func=AF.Silu), [z], [sil])
                V(lambda e, m=m: e.tensor_tensor(out=obc.a[:Tn, m * 256:(m + 1) * 256], in0=om.a[:Tn, :], in1=sil.a[:Tn, :], op=ALU.mult), [om, sil], [obc])
            D("sp", OBC.ap()[b * 128:b * 128 + Tn, :], obc.a[:Tn, :], r=[obc], w=[dummy()])

        S.barrier()
        reset_arena()
        BIAS = cb16(48 * 128, "BIAS")
        bst = cf32(2048, "bst")
        for i in range(3):
            D("sp", bst.a[:], c_bias.ap()[:, i * 2048:(i + 1) * 2048], w=[bst])
            V(lambda e, i=i: e.tensor_copy(out=BIAS.a[:, i * 2048:(i + 1) * 2048], in_=bst.a[:]), [bst], [BIAS])
        Qf = [cf32(512, "Qf0"), cf32(512, "Qf1")]
        KVf = [cf32(1024, "KVf0"), cf32(1024, "KVf1")]
        Qb = cb16(512, "Qb")
        Kb = cb16(512, "Kb")
        QTm = cb16(1024, "QTm")
        KT = [cb16(512, "KT0"), cb16(512, "KT1")]
        Vt = [cb16(8 * 65 + 8, "V0"), cb16(8 * 65 + 8, "V1")]
        PT = [cb16(512, "PT0"), cb16(512, "PT1")]
        OAt = [cf32(520, "OAt0"), cf32(520, "OAt1")]
        for i in range(2):
            G(lambda e, i=i: e.memset(Vt[i].a[:, 0:520], 1.0), [], [Vt[i]])
        tcount = 0
        for g, d in enumerate(DILS):
            nblk = NT // (d * 128)
            for r_ in range(d):
                for n in range(nblk):
                    base = r_ + d * 128 * n
                    qf = Qf[tcount % 2]
                    kvf = KVf[tcount % 2]
                    oat = OAt[tcount % 2]
                    cur = n % 2
                    prv = 1 - cur
                    tcount += 1
                    D("sp", qf.a[:], bass.AP(Z, base * 1536, [[d * 1536, 128], [1, 512]]), w=[qf])
                    D("sp", kvf.a[:], bass.AP(Z, base * 1536 + 512, [[d * 1536, 128], [1, 1024]]), w=[kvf])
                    A(lambda e, qf=qf: e.activation(out=Qb.a[:], in_=qf.a[:], func=AF.Copy, scale=0.125), [qf], [Qb])
                    G(lambda e, kvf=kvf: e.tensor_copy(out=Kb.a[:], in_=kvf.a[:, 0:512]), [kvf], [Kb])
                    G(lambda e, kvf=kvf, cur=cur: e.tensor_copy(out=Vt[cur].a[:, 0:520].rearrange("p (h e) -> p h e", h=8)[:, :, 0:64],
                                                                 in_=kvf.a[:, 512:1024].rearrange("p (h e) -> p h e", h=8)), [kvf], [Vt[cur]])
                    p = nbk()
                    for hp in range(4):
                        tr(p.a[:, hp * 128:(hp + 1) * 128], Qb.a[:, hp * 128:(hp + 1) * 128], identb.a[:], [Qb, identb], [p])
                    for e_ in range(2):
                        V(lambda e, p=p, e_=e_: e.tensor_scalar(out=QTm.a[:, :].rearrange("p (a b t) -> p a b t", a=4, b=2)[:, :, e_, :],
                                                               in0=p.a[:, 0:512].rearrange("p (a t) -> p a t", a=4),
                                                               scalar1=HALF[:, e_:e_ + 1], scalar2=None, op0=ALU.mult), [p, cf], [QTm])
                    p2 = nbk()
                    for hp in range(4):
                        tr(p2.a[:, hp * 128:(hp + 1) * 128], Kb.a[:, hp * 128:(hp + 1) * 128], identb.a[:], [Kb, identb], [p2])
                    A(lambda e, p2=p2, cur=cur: e.activation(out=KT[cur].a[:], in_=p2.a[:, 0:512], func=AF.Copy), [p2], [KT[cur]])
                    poA = nf()
                    poB = nf()
                    for hp in range(4):
                        ps = nf()
                        pt = PT[hp % 2]
                        for hh in range(2):
                            h = 2 * hp + hh
                            c0 = hh * 256
                            if n > 0:
                                mm(ps.a[:, c0:c0 + 128], KT[prv].a[:, hp * 128:(hp + 1) * 128], QTm.a[:, h * 128:(h + 1) * 128], True, False, [KT[prv], QTm], [ps])
                                bo = ((g * 2 + 0) * 8 + h) * 128
                                mm(ps.a[:, c0:c0 + 128], identb.a[:], BIAS.a[:, bo:bo + 128], False, True, [identb, BIAS], [ps])
                            mm(ps.a[:, c0 + 128:c0 + 256], KT[cur].a[:, hp * 128:(hp + 1) * 128], QTm.a[:, h * 128:(h + 1) * 128], True, False, [KT[cur], QTm], [ps])
                            bo = ((g * 2 + 1) * 8 + h) * 128
                            mm(ps.a[:, c0 + 128:c0 + 256], identb.a[:], BIAS.a[:, bo:bo + 128], False, True, [identb, BIAS], [ps])
                        if n > 0:
                            A(lambda e, ps=ps, pt=pt: e.activation(out=pt.a[:], in_=ps.a[:], func=AF.Exp), [ps], [pt])
                        else:
                            A(lambda e, ps=ps, pt=pt: e.activation(out=pt.a[:, :].rearrange("p (a c) -> p a c", a=2)[:, :, 128:256],
                                                                   in_=ps.a[:, :].rearrange("p (a c) -> p a c", a=2)[:, :, 128:256], func=AF.Exp), [ps], [pt])
                        for hh in range(2):
                            h = 2 * hp + hh
                            po = poA if h < 4 else poB
                            c = (h % 4) * 65
                            c0 = hh * 256
                            if n > 0:
                                mm(po.a[:, c:c + 65], pt.a[:, c0:c0 + 128], Vt[prv].a[:, h * 65:(h + 1) * 65], True, False, [pt, Vt[prv]], [po])
                                mm(po.a[:, c:c + 65], pt.a[:, c0 + 128:c0 + 256], Vt[cur].a[:, h * 65:(h + 1) * 65], False, True, [pt, Vt[cur]], [po])
                            else:
                                mm(po.a[:, c:c + 65], pt.a[:, c0 + 128:c0 + 256], Vt[cur].a[:, h * 65:(h + 1) * 65], True, True, [pt, Vt[cur]], [po])
                    A(lambda e, poA=poA, oat=oat: e.activation(out=oat.a[:, 0:260], in_=poA.a[:, 0:260], func=AF.Copy), [poA], [oat])
                    V(lambda e, poB=poB, oat=oat: e.tensor_copy(out=oat.a[:, 260:520], in_=poB.a[:, 0:260]), [poB], [oat])
                    D("pool", bass.AP(OA, (g * NT + base) * 520, [[d * 520, 128], [1, 520]]), oat.a[:], r=[oat], w=[dummy()])

        S.barrier()
        reset_arena()
        SBt = cb16(288, "SB")
        NBt = cb16(1024, "NB")
        bst = cf32(1024, "bst")
        D("sp", bst.a[:, 0:288], c_sb.ap(), w=[bst])
        V(lambda e: e.tensor_copy(out=SBt.a[:], in_=bst.a[:, 0:288]), [bst], [SBt])
        D("sp", bst.a[:], c_nb.ap(), w=[bst])
        V(lambda e: e.tensor_copy(out=NBt.a[:], in_=bst.a[:]), [bst], [NBt])
        Qf = cf32(512, "Qfs")
        KVf = [cf32(1024, "KVfs0"), cf32(1024, "KVfs1")]
        Qb = cb16(512, "Qbs")
        Kb = cb16(512, "Kbs")
        QTm = cb16(8 * 64, "QTms")
        KT = cb16(512, "KTs")
        Vt = cb16(528, "Vs")
        PTn = cb16(512, "PTn")
        PTs = cb16(NSQ * 512, "PTs")
        acc = cf32(520, "acc")
        rden = cf32(8, "rden")
        oas = cf32(512, "oas")
        G(lambda e: e.memset(Vt.a[:, 0:520], 1.0), [], [Vt])
        G(lambda e: e.memset(PTs.a[:], 0.0), [], [PTs])
        V(lambda e: e.memset(acc.a[:], 0.0), [], [acc])
        D("sp", Qf.a[0:TS, :], Z.ap()[NT:NT + TS, 0:512], w=[Qf])
        D("sp", KVf[0].a[0:TS, :], Z.ap()[NT:NT + TS, 512:1536], w=[KVf[0]])
        A(lambda e: e.activation(out=Qb.a[0:TS, :], in_=Qf.a[0:TS, :], func=AF.Copy, scale=0.125), [Qf], [Qb])
        G(lambda e: e.tensor_copy(out=Kb.a[0:TS, :], in_=KVf[0].a[0:TS, 0:512]), [KVf[0]], [Kb])
        G(lambda e: e.tensor_copy(out=Vt.a[0:TS, 0:520].rearrange("p (h e) -> p h e", h=8)[:, :, 0:64],
                                  in_=KVf[0].a[0:TS, 512:1024].rearrange("p (h e) -> p h e", h=8)), [KVf[0]], [Vt])
        p = nbk()
        for hp in range(4):
            tr(p.a[:, hp * 64:(hp + 1) * 64], Qb.a[0:TS, hp * 128:(hp + 1) * 128], identb.a[:TS, :TS], [Qb, identb], [p])
        for e_ in range(2):
            V(lambda e, p=p, e_=e_: e.tensor_scalar(out=QTm.a[:, :].rearrange("p (a b t) -> p a b t", a=4, b=2)[:, :, e_, :],
                                                   in0=p.a[:, 0:256].rearrange("p (a t) -> p a t", a=4),
                                                   scalar1=HALF[:, e_:e_ + 1], scalar2=None, op0=ALU.mult), [p, cf], [QTm])
        p2 = nbk()
        for hp in range(4):
            tr(p2.a[:, hp * 64:(hp + 1) * 64], Kb.a[0:TS, hp * 128:(hp + 1) * 128], identb.a[:TS, :TS], [Kb, identb], [p2])
        A(lambda e, p2=p2: e.activation(out=KT.a[:, 0:256], in_=p2.a[:, 0:256], func=AF.Copy), [p2], [KT])
        for ps_i in range(2):
            ps = nf()
            for h in range(8):
                mm(ps.a[:TS, h * 64:(h + 1) * 64], KT.a[:, (h // 2) * 64:(h // 2 + 1) * 64], QTm.a[:, h * 64:(h + 1) * 64], True, False, [KT, QTm], [ps])
                bo = (ps_i * 8 + h) * 64
                mm(ps.a[:TS, h * 64:(h + 1) * 64], identb.a[:TS, :TS], NBt.a[:TS, bo:bo + 64], False, True, [identb, NBt], [ps])
            A(lambda e, ps=ps: e.activation(out=PTn.a[:TS, :], in_=ps.a[:TS, :], func=AF.Exp), [ps], [PTn])
            poA = nf()
            poB = nf()
            for h in range(8):
                po = poA if h < 4 else poB
                c = (h % 4) * 65
                mm(po.a[:TS, c:c + 65], PTn.a[:TS, h * 64:(h + 1) * 64], Vt.a[:TS, h * 65:(h + 1) * 65], True, True, [PTn, Vt], [po])
            wgt = 1.0 if ps_i == 0 else 2.0
            V(lambda e, poA=poA, wgt=wgt: e.scalar_tensor_tensor(out=acc.a[:TS, 0:260], in0=poA.a[:TS, 0:260], scalar=wgt, in1=acc.a[:TS, 0:260],
                                                                 op0=ALU.mult, op1=ALU.add), [poA, acc], [acc])
            V(lambda e, poB=poB, wgt=wgt: e.scalar_tensor_tensor(out=acc.a[:TS, 260:520], in0=poB.a[:TS, 0:260], scalar=wgt, in1=acc.a[:TS, 260:520],
                                                                 op0=ALU.mult, op1=ALU.add), [poB, acc], [acc])
        Vc = [cb16(528, "Vc0"), cb16(528, "Vc1")]
        KTc = [cb16(512, "KTc0"), cb16(512, "KTc1")]
        Kbc = [cb16(512, "Kbc0"), cb16(512, "Kbc1")]
        for i in range(2):
            G(lambda e, i=i: e.memset(Vc[i].a[:, 0:520], 1.0), [], [Vc[i]])
        tcount = 0
        for s in range(NSQ):
            for ty in range(9):
                if ty == 0:
                    row0, st_ = 1920, 1
                elif ty <= 4:
                    row0, st_ = 1536 + (ty - 1), 4
                else:
                    row0, st_ = (ty - 5), 16
                i2 = tcount % 2
                tcount += 1
                kvf = KVf[i2]
                D("sp" if tcount % 2 else "act", kvf.a[:], bass.AP(cache, ((l * NSQ + s) * 2048 + row0) * 1024, [[st_ * 1024, 128], [1, 1024]]), w=[kvf])
                G(lambda e, kvf=kvf, i2=i2: e.tensor_copy(out=Kbc[i2].a[:], in_=kvf.a[:, 0:512]), [kvf], [Kbc[i2]])
                G(lambda e, kvf=kvf, i2=i2: e.tensor_copy(out=Vc[i2].a[:, 0:520].rearrange("p (h e) -> p h e", h=8)[:, :, 0:64],
                                                          in_=kvf.a[:, 512:1024].rearrange("p (h e) -> p h e", h=8)), [kvf], [Vc[i2]])
                p2 = nbk()
                for hp in range(4):
                    tr(p2.a[:, hp * 128:(hp + 1) * 128], Kbc[i2].a[:, hp * 128:(hp + 1) * 128], identb.a[:], [Kbc[i2], identb], [p2])
                evac(KTc[i2].a[:], p2.a[:, 0:512], [p2], [KTc[i2]])
                ps = nf()
                for h in range(8):
                    mm(ps.a[:, h * 4:(h + 1) * 4], KTc[i2].a[:, (h // 2) * 128:(h // 2 + 1) * 128], QTm.a[:, h * 64 + 4 * s:h * 64 + 4 * s + 4], True, False, [KTc[i2], QTm], [ps])
                    bo = (ty * 8 + h) * 4
                    mm(ps.a[:, h * 4:(h + 1) * 4], identb.a[:], SBt.a[:, bo:bo + 4], False, True, [identb, SBt], [ps])
                A(lambda e, ps=ps, s=s: e.activation(out=PTs.a[:, s * 512:(s + 1) * 512].rearrange("p (h t) -> p h t", h=8)[:, :, 4 * s:4 * s + 4],
                                                     in_=ps.a[:, 0:32].rearrange("p (h t) -> p h t", h=8), func=AF.Exp), [ps], [PTs])
                poA = nf()
                poB = nf()
                for h in range(8):
                    po = poA if h < 4 else poB
                    c = (h % 4) * 65
                    mm(po.a[:TS, c:c + 65], PTs.a[:, s * 512 + h * 64:s * 512 + (h + 1) * 64], Vc[i2].a[:, h * 65:(h + 1) * 65], True, True, [PTs, Vc[i2]], [po])
                V(lambda e, poA=poA: e.tensor_tensor(out=acc.a[:TS, 0:260], in0=poA.a[:TS, 0:260], in1=acc.a[:TS, 0:260], op=ALU.add), [poA, acc], [acc])
                V(lambda e, poB=poB: e.tensor_tensor(out=acc.a[:TS, 260:520], in0=poB.a[:TS, 0:260], in1=acc.a[:TS, 260:520], op=ALU.add), [poB, acc], [acc])
        acc3 = acc.a[:TS, :].rearrange("p (h e) -> p h e", h=8)
        V(lambda e: e.reciprocal(out=rden.a[:TS, :].unsqueeze(2), in_=acc3[:, :, 64:65]), [acc], [rden])
        V(lambda e: e.tensor_tensor(out=oas.a[:TS, :].rearrange("p (h e) -> p h e", h=8), in0=acc3[:, :, 0:64],
                                    in1=rden.a[:TS, :].unsqueeze(2).to_broadcast([TS, 8, 64]), op=ALU.mult), [acc, rden], [oas])
        D("sp", OAS.ap()[0:TS, :], oas.a[:TS, :], r=[oas], w=[dummy()])
        D("sp", OAS.ap()[TS:128, :], oas.a[:TS, :], r=[oas], w=[dummy()])

        S.barrier()
        reset_arena()
        w_out_v = wbig.a[:, 0:8192].rearrange("p (k c) -> p k c", k=8)
        w_pq_v = wbig.a[:, 8192:8192 + 16384].rearrange("p (k c) -> p k c", k=8)
        load_weight(w_out, l, 1024, w_out_v)
        load_weight(w_pq, l, 2048, w_pq_v)
        gffn = cf32(1024, "gffn")
        D("sp", gffn.a[:], g_ffn.ap()[l].partition_broadcast(128), w=[gffn])
        if l == 1:
            gfin = cf32(1024, "gfin")
            D("sp", gfin.a[:], g_fin.ap().partition_broadcast(128), w=[gfin])
        KEYS = cf32(256, "KEYS")
        D("sp", KEYS.a[:, :].rearrange("p (a k) -> p a k", a=2), keysT.ap()[l].rearrange("a c k -> c a k"), w=[KEYS])
        xt = cf32(1024, "xD")
        of = cf32(1024, "of")
        oa3 = cf32(3 * 520, "oa3")
        rden = cf32(8, "rdenD")
        ob = cb16(1024, "ob")
        oT = cb16(1024, "oT")
        hn2f = cf32(1024, "hn2f")
        hn2b = cb16(1024, "hn2b")
        hn2T = cb16(1024, "hn2T")
        qkTf = cf32(2048, "qkTf")
        sc = cf32(2048, "sc")
        sv = cf32(256, "sv")
        si = T(sb("si%d" % l, [128, 256], U32), "si")
        sif = cf32(256, "sif")
        wk = cf32(256, "wk")
        cand = cf32(256, "cand")
        cidx = cf32(256, "cidx")
        ts_ = cf32(128, "ts")
        eq8 = cf32(2048, "eq8")
        eif = cf32(128, "eif")
        ei = T(sb("ei%d" % l, [128, 128], I32), "ei")
        pos = T(sb("pos%d" % l, [128, 128], U32), "pos")
        posf = cf32(128, "posf")
        gt = cf32(128, "gt")
        gs = cf32(8, "gs")
        pre = cf32(128, "pre")
        ge = cf32(128, "ge")
        aj = cf32(128, "aj")
        accv = cf32(1024, "accv")
        junkD = cf32(1024, "junkD")
        ssD = cf32(1, "ssD")
        Gb = [cf32(2048, "G%d" % i) for i in range(3)]
        uvl = uvs[l].ap()
        for b in range(NB + 1):
            samp = (b == NB)
            if l == 0:
                src = xs.ap() if samp else xp.ap()[b * 128:(b + 1) * 128, :]
            else:
                src = XR.ap()[b * 128:(b + 1) * 128, :]
            D("sp", xt.a[:], src, w=[xt])
            if samp:
                D("sp", of.a[:, 0:512], OAS.ap(), w=[of])
            else:
                D("sp", oa3.a[:, :].rearrange("p (g c) -> p g c", g=3), bass.AP(OA, b * 128 * 520, [[520, 128], [NT * 520, 3], [1, 520]]), w=[oa3])
                V(lambda e: e.tensor_tensor(out=oa3.a[:, 0:520], in0=oa3.a[:, 0:520], in1=oa3.a[:, 520:1040], op=ALU.add), [oa3], [oa3])
                V(lambda e: e.tensor_tensor(out=oa3.a[:, 0:520], in0=oa3.a[:, 0:520], in1=oa3.a[:, 1040:1560], op=ALU.add), [oa3], [oa3])
                s3 = oa3.a[:, 0:520].rearrange("p (h e) -> p h e", h=8)
                V(lambda e, s3=s3: e.reciprocal(out=rden.a[:, :].unsqueeze(2), in_=s3[:, :, 64:65]), [oa3], [rden])
                V(lambda e, s3=s3: e.tensor_tensor(out=of.a[:, 0:512].rearrange("p (h e) -> p h e", h=8), in0=s3[:, :, 0:64],
                                                   in1=rden.a[:, :].unsqueeze(2).to_broadcast([128, 8, 64]), op=ALU.mult), [oa3, rden], [of])
            if samp:
                D("sp", of.a[0:TS, 512:1024], OBC.ap()[NT:NT + TS, :], w=[of])
                D("sp", of.a[TS:128, 512:1024], OBC.ap()[NT:NT + TS, :], w=[of])
            else:
                D("sp", of.a[:, 512:1024], OBC.ap()[b * 128:(b + 1) * 128, :], w=[of])
            A(lambda e: e.activation(out=ob.a[:], in_=of.a[:], func=AF.Copy), [of], [ob])
            transpose8(ob, oT)
            oTv = oT.a[:, :].rearrange("p (k t) -> p k t", k=8)
            for n in range(2):
                p = nf()
                for kc in range(8):
                    mm(p.a[:, :], oTv[:, kc, :], w_out_v[:, kc, n * 512:(n + 1) * 512], kc == 0, kc == 7, [oT, wbig], [p])
                V(lambda e, p=p, n=n: e.tensor_tensor(out=xt.a[:, n * 512:(n + 1) * 512], in0=p.a[:, :], in1=xt.a[:, n * 512:(n + 1) * 512], op=ALU.add), [p, xt], [xt])
            rmsnorm(xt, gffn, [(hn2f, hn2f.a[:])], junkD, ssD)
            A(lambda e: e.activation(out=hn2b.a[:], in_=hn2f.a[:], func=AF.Copy), [hn2f], [hn2b])
            transpose8(hn2b, hn2T)
            hTv = hn2T.a[:, :].rearrange("p (k t) -> p k t", k=8)
            for n in range(4):
                p = nf()
                for kc in range(8):
                    mm(p.a[:, :], hTv[:, kc, :], w_pq_v[:, kc, n * 512:(n + 1) * 512], kc == 0, kc == 7, [hn2T, wbig], [p])
                evac(z.a[:, n * 512:(n + 1) * 512], p.a[:, :], [p], [z])
            for n in range(4):
                p = nf()
                for j in range(4):
                    g_ = n * 4 + j
                    tr(p.a[:, j * 128:(j + 1) * 128], z.a[:, g_ * 128:(g_ + 1) * 128], identf, [z, cf], [p])
                evac(qkTf.a[:, n * 512:(n + 1) * 512], p.a[:, :], [p], [qkTf])
            for n in range(4):
                p = nf()
                for j in range(4):
                    g_ = n * 4 + j
                    mm(p.a[:, j * 128:(j + 1) * 128], qkTf.a[:, g_ * 128:(g_ + 1) * 128], KEYS.a[:, (g_ % 2) * 128:(g_ % 2 + 1) * 128], True, True, [qkTf, KEYS], [p])
                evac(sc.a[:, n * 512:(n + 1) * 512], p.a[:, :], [p], [sc])
            for g_ in range(16):
                scg = sc.a[:, g_ * 128:(g_ + 1) * 128]
                V(lambda e, g_=g_, scg=scg: e.max(out=sv.a[:, g_ * 16:g_ * 16 + 8], in_=scg), [sc], [sv])
                V(lambda e, g_=g_, scg=scg: e.max_index(out=si.a[:, g_ * 16:g_ * 16 + 8], in_max=sv.a[:, g_ * 16:g_ * 16 + 8], in_values=scg), [sc, sv], [si])
                V(lambda e, g_=g_, scg=scg: e.match_replace(out=wk.a[:, 0:128], in_to_replace=sv.a[:, g_ * 16:g_ * 16 + 8], in_values=scg, imm_value=-1e30), [sc, sv], [wk])
                V(lambda e, g_=g_: e.max(out=sv.a[:, g_ * 16 + 8:g_ * 16 + 16], in_=wk.a[:, 0:128]), [wk], [sv])
                V(lambda e, g_=g_: e.max_index(out=si.a[:, g_ * 16 + 8:g_ * 16 + 16], in_max=sv.a[:, g_ * 16 + 8:g_ * 16 + 16], in_values=wk.a[:, 0:128]), [wk, sv], [si])
            V(lambda e: e.tensor_copy(out=sif.a[:], in_=si.a[:]), [si], [sif])
            for h in range(8):
                a0 = (2 * h) * 16
                b0 = (2 * h + 1) * 16
                V(lambda e, a0=a0, b0=b0: e.tensor_tensor(out=cand.a[:, :].rearrange("p (r q) -> p r q", r=16),
                                                         in0=sv.a[:, a0:a0 + 16].unsqueeze(2).to_broadcast([128, 16, 16]),
                                                         in1=sv.a[:, b0:b0 + 16].unsqueeze(1).to_broadcast([128, 16, 16]), op=ALU.add), [sv], [cand])
                V(lambda e, a0=a0, b0=b0: e.scalar_tensor_tensor(out=cidx.a[:, :].rearrange("p (r q) -> p r q", r=16),
                                                                in0=sif.a[:, a0:a0 + 16].unsqueeze(2).to_broadcast([128, 16, 16]), scalar=128.0,
                                                                in1=sif.a[:, b0:b0 + 16].unsqueeze(1).to_broadcast([128, 16, 16]),
                                                                op0=ALU.mult, op1=ALU.add), [sif], [cidx])
                V(lambda e, h=h: e.max(out=ts_.a[:, h * 16:h * 16 + 8], in_=cand.a[:]), [cand], [ts_])
                V(lambda e, h=h: e.max_index(out=pos.a[:, h * 16:h * 16 + 8], in_max=ts_.a[:, h * 16:h * 16 + 8], in_values=cand.a[:]), [cand, ts_], [pos])
                V(lambda e, h=h: e.match_replace(out=wk.a[:], in_to_replace=ts_.a[:, h * 16:h * 16 + 8], in_values=cand.a[:], imm_value=-1e30), [cand, ts_], [wk])
                V(lambda e, h=h: e.max(out=ts_.a[:, h * 16 + 8:h * 16 + 16], in_=wk.a[:]), [wk], [ts_])
                V(lambda e, h=h: e.max_index(out=pos.a[:, h * 16 + 8:h * 16 + 16], in_max=ts_.a[:, h * 16 + 8:h * 16 + 16], in_values=wk.a[:]), [wk, ts_], [pos])
                V(lambda e, h=h: e.tensor_copy(out=posf.a[:, h * 16:(h + 1) * 16], in_=pos.a[:, h * 16:(h + 1) * 16]), [pos], [posf])
                for hf in range(2):
                    k0 = h * 16 + hf * 8
                    e3 = eq8.a[:, :].rearrange("p (k c) -> p k c", k=8)
                    V(lambda e, k0=k0, e3=e3: e.tensor_tensor(out=e3, in0=cf.a[:, C_IOTA:C_IOTA + 256].unsqueeze(1).to_broadcast([128, 8, 256]),
                                                             in1=posf.a[:, k0:k0 + 8].unsqueeze(2).to_broadcast([128, 8, 256]), op=ALU.is_equal), [cf, posf], [eq8])
                    V(lambda e, e3=e3: e.tensor_tensor(out=e3, in0=e3, in1=cidx.a[:, :].unsqueeze(1).to_broadcast([128, 8, 256]), op=ALU.mult), [eq8, cidx], [eq8])
                    V(lambda e, k0=k0, e3=e3: e.tensor_reduce(out=eif.a[:, k0:k0 + 8], in_=e3, axis=AX.X, op=ALU.add), [eq8], [eif])
            ts3 = ts_.a[:, :].rearrange("p (h k) -> p h k", h=8)
            gt3 = gt.a[:, :].rearrange("p (h k) -> p h k", h=8)
            V(lambda e: e.tensor_tensor(out=gt3, in0=ts3, in1=ts3[:, :, 0:1].to_broadcast([128, 8, 16]), op=ALU.subtract), [ts_], [gt])
            A(lambda e: e.activation(out=gt.a[:], in_=gt.a[:], func=AF.Exp), [gt], [gt])
            V(lambda e: e.tensor_reduce(out=gs.a[:, :], in_=gt3, axis=AX.X, op=ALU.add), [gt], [gs])
            V(lambda e: e.reciprocal(out=gs.a[:], in_=gs.a[:]), [gs], [gs])
            V(lambda e: e.tensor_tensor(out=gt3, in0=gt3, in1=gs.a[:, :].unsqueeze(2).to_broadcast([128, 8, 16]), op=ALU.mult), [gt, gs], [gt])
            V(lambda e: e.tensor_copy(out=ei.a[:], in_=eif.a[:]), [eif], [ei])
            V(lambda e: e.memset(pre.a[:], 0.0), [], [pre])
            V(lambda e: e.memset(accv.a[:], 0.0), [], [accv])
            LAG = 1
            def vacc(j):
                gb_ = Gb[j % 3]
                V(lambda e, j=j, gb_=gb_: e.scalar_tensor_tensor(out=accv.a[:], in0=gb_.a[:, 1024:2048], scalar=aj.a[:, j:j + 1], in1=accv.a[:],
                                                                op0=ALU.mult, op1=ALU.add), [gb_, aj, accv], [accv])
            for j in range(128):
                gb_ = Gb[j % 3]
                S.dma("pool", lambda e, j=j, gb_=gb_: e.indirect_dma_start(
                    out=gb_.a[:, :], out_offset=None, in_=uvl,
                    in_offset=bass.IndirectOffsetOnAxis(ap=ei.a[:, j:j + 1], axis=0)),
                    [ei.r], [gb_.r])
                V(lambda e, j=j, gb_=gb_: e.scalar_tensor_tensor(out=junkD.a[:], in0=hn2f.a[:], scalar=1.0, in1=gb_.a[:, 0:1024],
                                                                op0=ALU.mult, op1=ALU.mult, accum_out=pre.a[:, j:j + 1]), [hn2f, gb_], [junkD, pre])
                A(lambda e, j=j: e.activation(out=ge.a[:, j:j + 1], in_=pre.a[:, j:j + 1], func=AF.Gelu), [pre], [ge])
                A(lambda e, j=j: e.activation(out=aj.a[:, j:j + 1], in_=ge.a[:, j:j + 1], func=AF.Copy, scale=gt.a[:, j:j + 1]), [ge, gt], [aj])
                if j >= LAG:
                    vacc(j - LAG)
            for j in range(128 - LAG, 128):
                vacc(j)
            V(lambda e: e.tensor_tensor(out=xt.a[:], in0=xt.a[:], in1=accv.a[:], op=ALU.add), [xt, accv], [xt])
            if l == 0:
                D("sp", XR.ap()[b * 128:(b + 1) * 128, :], xt.a[:], r=[xt], w=[dummy()])
            else:
                rmsnorm(xt, gfin, [(of, of.a[:])], junkD, ssD)
                if samp:
                    D("sp", y_s.ap(), of.a[0:TS, :], r=[of], w=[dummy()])
                else:
                    D("sp", y_p.ap()[b * 128:(b + 1) * 128, :], of.a[:], r=[of], w=[dummy()])

    S.barrier()
    S.emit()
    S.close()
    es.close()
    return nc, S


_CACHE = {}


def run(inputs, SEQ, ncores, trace=False):
    NB = SEQ // 128
    x_prompt = np.asarray(inputs["x_prompt"], np.float32)
    x_sample = np.asarray(inputs["x_sample"], np.float32)
    cache_kv = np.asarray(inputs["cache_kv_win"], np.float32)
    state_gla = np.asarray(inputs["state_gla"], np.float32)
    state_ret = np.asarray(inputs["state_ret"], np.float32)
    nbatch = x_prompt.shape[0]
    c, bias, sbt, nbt, seqcol = make_consts()
    u_tab = np.asarray(inputs["u_tab"], np.float32)
    v_tab = np.asarray(inputs["v_tab"], np.float32)
    uvt = np.concatenate([u_tab, v_tab], axis=-1)
    keysT = np.ascontiguousarray(np.asarray(inputs["sub_keys"], np.float32).transpose(0, 1, 3, 2))
    common = {
        "w_in": np.asarray(inputs["w_in"], np.float32),
        "w_g2": np.asarray(inputs["w_gate2"], np.float32),
        "b_g": np.asarray(inputs["b_gate"], np.float32).reshape(2, 1, 128),
        "g_gla": np.asarray(inputs["g_gla"], np.float32),
        "w_out": np.asarray(inputs["w_out"], np.float32),
        "g_mix": np.asarray(inputs["g_mix"], np.float32),
        "g_ffn": np.asarray(inputs["g_ffn"], np.float32),
        "w_pq": np.asarray(inputs["w_pq"], np.float32),
        "keysT": keysT,
        "uv0": np.ascontiguousarray(uvt[0]), "uv1": np.ascontiguousarray(uvt[1]),
        "g_fin": np.asarray(inputs["g_final"], np.float32),
        "c_f32": c, "c_bias": bias, "c_sb": sbt, "c_nb": nbt, "c_sc": seqcol,
    }
    in_maps = []
    for ci in range(ncores):
        xs = x_sample[ci * NSQ:(ci + 1) * NSQ].reshape(TS, DM)
        xs2 = np.concatenate([xs, xs], axis=0)
        m = dict(common)
        m["xp"] = np.ascontiguousarray(x_prompt[ci % nbatch])
        m["xs"] = np.ascontiguousarray(xs2)
        m["cache"] = np.ascontiguousarray(cache_kv[:, ci * NSQ:(ci + 1) * NSQ].reshape(2, NSQ, 2048, 1024))
        m["sgla"] = np.ascontiguousarray(state_gla[:, ci * NSQ:(ci + 1) * NSQ].reshape(2, NSQ, 128, 64))
        m["sret"] = np.ascontiguousarray(state_ret[:, ci * NSQ:(ci + 1) * NSQ].reshape(2, NSQ, 128, 64))
        in_maps.append(m)
    if NB not in _CACHE:
        _CACHE[NB] = build(NB)[0]
    nc = _CACHE[NB]
    res = run_bass_kernel_spmd(nc, in_maps, core_ids=list(range(ncores)), **({"trace": True} if trace else {}))
    R = res.results
    y_prompt = np.stack([R[b]["y_p"] for b in range(nbatch)], 0).reshape(nbatch, SEQ, DM)
    y_sample = np.concatenate([R[ci]["y_s"].reshape(NSQ, 4, DM) for ci in range(ncores)], 0)
    kv_p = np.stack([np.stack([R[b]["kvp"][l] for b in range(nbatch)], 0) for l in range(2)], 0).reshape(2, nbatch, 2048, 2, 8, 64)
    kv_s = np.stack([np.concatenate([R[ci]["kvs"][l].reshape(NSQ, 4, 2, 8, 64) for ci in range(ncores)], 0) for l in range(2)], 0)
    gla_p = np.stack([np.stack([R[b]["glap"][l] for b in range(nbatch)], 0) for l in range(2)], 0).reshape(2, nbatch, 4, 32, 64)
    ret_p = np.stack([np.stack([R[b]["retp"][l] for b in range(nbatch)], 0) for l in range(2)], 0).reshape(2, nbatch, 4, 32, 64)
    gla_s = np.stack([np.concatenate([R[ci]["glas"][l] for ci in range(ncores)], 0) for l in range(2)], 0).reshape(2, ncores * NSQ, 4, 32, 64)
    ret_s = np.stack([np.concatenate([R[ci]["rets"][l] for ci in range(ncores)], 0) for l in range(2)], 0).reshape(2, ncores * NSQ, 4, 32, 64)
    outs = (y_prompt, y_sample, kv_p, kv_s, gla_p, gla_s, ret_p, ret_s)
    return tuple(np.ascontiguousarray(o, dtype=np.float32) for o in outs)


def kernel(**inputs):
    return run(inputs, 8192, NCORES)
```

```python
import math
from contextlib import ExitStack
import numpy as np
import concourse.bass as bass
import concourse.mybir as mybir
from concourse.bass_utils import run_bass_kernel_spmd

F32 = mybir.dt.float32
BF16 = mybir.dt.bfloat16
I32 = mybir.dt.int32
U32 = mybir.dt.uint32
AF = mybir.ActivationFunctionType
ALU = mybir.AluOpType
AX = mybir.AxisListType

NCORES = 8
DM = 1024
DIN = 3088
NEG = -30000.0
EPS = 1e-6
SLOPES = [2.0 ** (-8.0 * (i + 1) / 8) for i in range(8)]
DILS = (1, 4, 16)
NSQ = 16
TS = 64

EPOCH = 24000
DMA_EPOCH = 1500


class Res:
    __slots__ = ("name", "w", "r")

    def __init__(self, name=""):
        self.name = name
        self.w = None
        self.r = []


class _Rec:
    def __init__(self):
        self.call = None

    def __getattr__(self, name):
        def f(*a, **k):
            self.call = (name, a, k)
            return self
        return f


def _record(fn):
    rec = _Rec()
    fn(rec)
    assert rec.call is not None
    return rec.call


class Sched:
    ENG = ("pe", "act", "dve", "pool", "sp")

    def __init__(self, nc):
        self.nc = nc
        self.lists = {e: [] for e in self.ENG}
        self.tick = {}
        self.sems = {}
        self.seen = {e: {} for e in self.ENG}
        self.dma_slots = {"sp": 8, "act": 4, "pool": 8}
        self.dma_n = {q: 0 for q in self.dma_slots}
        self.ninstr = 0
        self._semctx = []

    def _sem(self, key, epoch):
        k = (key, epoch)
        if k not in self.sems:
            name = "s%d" % len(self.sems)
            cm = self.nc.semaphore(name)
            self._semctx.append(cm)
            self.sems[k] = cm.__enter__()
        return self.sems[k]

    def _locate(self, key, tick):
        if isinstance(key, tuple):
            ep = (tick - 1) // DMA_EPOCH
            return self._sem(key, ep), 16 * (tick - ep * DMA_EPOCH)
        ep = (tick - 1) // EPOCH
        return self._sem(key, ep), tick - ep * EPOCH

    def _wait(self, eng, key, tick):
        if self.seen[eng].get(key, 0) >= tick:
            return
        self.seen[eng][key] = tick
        sem, val = self._locate(key, tick)
        self.lists[eng].append(("wait", sem, val))

    def _deps(self, eng, reads, writes):
        need = {}
        for r in reads:
            if r.w is not None:
                k, t = r.w
                if need.get(k, 0) < t:
                    need[k] = t
        for w in writes:
            if w.w is not None:
                k, t = w.w
                if need.get(k, 0) < t:
                    need[k] = t
            for (k, t) in w.r:
                if need.get(k, 0) < t:
                    need[k] = t
        for k, t in need.items():
            self._wait(eng, k, t)

    def _mark(self, key, tick, reads, writes):
        for r in reads:
            r.r.append((key, tick))
            if len(r.r) > 48:
                d = {}
                for (k, t) in r.r:
                    if d.get(k, 0) < t:
                        d[k] = t
                r.r = list(d.items())
        for w in writes:
            w.w = (key, tick)
            w.r = []

    def op(self, eng, fn, reads=(), writes=()):
        self._deps(eng, reads, writes)
        t = self.tick.get(eng, 0) + 1
        self.tick[eng] = t
        sem, _ = self._locate(eng, t)
        self.lists[eng].append(("op", _record(fn), sem, 1))
        self._mark(eng, t, reads, writes)
        self.ninstr += 1

    def dma(self, q, fn, reads=(), writes=()):
        n = self.dma_n[q]
        self.dma_n[q] = n + 1
        ns = self.dma_slots[q]
        slot = n % ns
        key = ("d", q, slot)
        t = n // ns + 1
        if t > 1:
            self._wait(q, key, t - 1)
        self._deps(q, reads, writes)
        sem, _ = self._locate(key, t)
        self.lists[q].append(("op", _record(fn), sem, 16))
        self.tick[key] = t
        self._mark(key, t, reads, writes)
        self.ninstr += 1

    def wait_all(self, eng):
        for key, t in list(self.tick.items()):
            if t > 0:
                self._wait(eng, key, t)

    def barrier(self):
        for e in self.ENG:
            self.wait_all(e)

    def emit(self):
        nc = self.nc
        lists = self.lists

        def play(engobj, items):
            for it in items:
                if it[0] == "wait":
                    engobj.wait_ge(it[1], it[2])
                else:
                    name, a, k = it[1]
                    ins = getattr(engobj, name)(*a, **k)
                    ins.then_inc(it[2], it[3])

        with nc.Block() as block:
            @block.sync
            def _(e):
                play(e, lists["sp"])

            @block.scalar
            def _(e):
                play(e, lists["act"])

            @block.vector
            def _(e):
                play(e, lists["dve"])

            @block.gpsimd
            def _(e):
                play(e, lists["pool"])

            @block.tensor
            def _(e):
                play(e, lists["pe"])

    def close(self):
        for cm in reversed(self._semctx):
            cm.__exit__(None, None, None)
        self._semctx = []


class T:
    __slots__ = ("a", "r")

    def __init__(self, a, name=""):
        self.a = a
        self.r = Res(name)


C_ID = 0
C_TRI = 128
C_ONE = 256
C_SCUM = 384
C_SSAME = 512
C_LGAM = 640
C_SEQI = 768
C_HM = 784
C_HALF = 788
C_ONEC = 790
C_IOTA = 792
NCF = 792 + 256


def make_consts():
    c = np.zeros((128, NCF), np.float32)
    p = np.arange(128)
    c[:, C_ID:C_ID + 128] = np.eye(128, dtype=np.float32)
    c[:, C_TRI:C_TRI + 128] = (p[:, None] <= p[None, :]).astype(np.float32)
    c[:, C_ONE:C_ONE + 128] = 1.0
    same = (p[:, None] // 4 == p[None, :] // 4) & (p[:, None] < TS) & (p[None, :] < TS)
    c[:, C_SCUM:C_SCUM + 128] = (same & (p[:, None] <= p[None, :])).astype(np.float32)
    c[:, C_SSAME:C_SSAME + 128] = same.astype(np.float32)
    lg = np.log(1.0 - 2.0 ** (-5.0 - np.arange(4, dtype=np.float64))).astype(np.float32)
    c[:, C_LGAM:C_LGAM + 128] = np.repeat(lg, 32)[None, :]
    c[:, C_SEQI:C_SEQI + 16] = ((p[:, None] // 4 == np.arange(16)[None, :]) & (p[:, None] < TS)).astype(np.float32)
    c[:, C_HM:C_HM + 4] = (p[:, None] // 32 == np.arange(4)[None, :]).astype(np.float32)
    c[:, C_HALF:C_HALF + 2] = (p[:, None] // 64 == np.arange(2)[None, :]).astype(np.float32)
    c[:, C_ONEC] = 1.0
    c[:, C_IOTA:C_IOTA + 256] = np.arange(256, dtype=np.float32)[None, :]
    bias = np.zeros((128, 3, 2, 8, 128), np.float32)
    k = np.arange(128)[:, None]
    q = np.arange(128)[None, :]
    for g, d in enumerate(DILS):
        for h in range(8):
            dist = q + 128 - k
            b = np.where(k >= q, -SLOPES[h] * dist * d, NEG)
            bias[:, g, 0, h, :] = b
            dist = q - k
            b = np.where(k <= q, -SLOPES[h] * dist * d, NEG)
            bias[:, g, 1, h, :] = b
    bias = bias.reshape(128, 48 * 128)
    sb = np.full((128, 9, 8, 4), NEG, np.float32)
    i = np.arange(128)
    for h in range(8):
        for t in range(4):
            dist = 128 + t - i
            sb[:, 0, h, t] = np.where(i >= t, -SLOPES[h] * dist, NEG)
            sb[:, 1 + t, h, t] = -SLOPES[h] * (512 - 4 * i)
            sb[:, 5 + t, h, t] = -SLOPES[h] * (2048 - 16 * i)
    sb = sb.reshape(128, 288)
    nb = np.full((128, 2, 8, 64), NEG, np.float32)
    kt = np.arange(64)[:, None]
    qt = np.arange(64)[None, :]
    sameq = (kt // 4 == qt // 4)
    for h in range(8):
        nb[:64, 0, h, :] = np.where(sameq & (kt <= qt), -SLOPES[h] * (qt - kt), NEG)
        nb[:64, 1, h, :] = np.where(kt == qt, 0.0, NEG)
    nb = nb.reshape(128, 1024)
    seqcol = np.zeros((128, 16, 64), np.float32)
    for s in range(16):
        seqcol[:, s, 4 * s:4 * s + 4] = 1.0
    seqcol = seqcol.reshape(128, 1024)
    return c, bias, sb, nb, seqcol


def build(NB):
    NT = NB * 128
    NR = NT + 128
    KVB0 = NB - 16
    nc = bass.Bass("TRN2", target_bir_lowering=False)
    dt_in = lambda n, s, d=F32: nc.dram_tensor(n, list(s), d, kind="ExternalInput")
    dt_out = lambda n, s, d=F32: nc.dram_tensor(n, list(s), d, kind="ExternalOutput")
    xp = dt_in("xp", [NT, DM])
    xs = dt_in("xs", [128, DM])
    cache = dt_in("cache", [2, NSQ, 2048, 1024])
    sgla = dt_in("sgla", [2, NSQ, 128, 64])
    sret = dt_in("sret", [2, NSQ, 128, 64])
    w_in = dt_in("w_in", [2, DM, DIN])
    w_g2 = dt_in("w_g2", [2, 16, 128])
    b_g = dt_in("b_g", [2, 1, 128])
    g_gla = dt_in("g_gla", [2, 64])
    w_out = dt_in("w_out", [2, DM, DM])
    g_mix = dt_in("g_mix", [2, DM])
    g_ffn = dt_in("g_ffn", [2, DM])
    w_pq = dt_in("w_pq", [2, DM, 2048])
    keysT = dt_in("keysT", [2, 2, 128, 128])
    uvs = [dt_in("uv0", [16384, 2048]), dt_in("uv1", [16384, 2048])]
    g_fin = dt_in("g_fin", [DM])
    c_f32 = dt_in("c_f32", [128, NCF])
    c_bias = dt_in("c_bias", [128, 48 * 128])
    c_sb = dt_in("c_sb", [128, 288])
    c_nb = dt_in("c_nb", [128, 1024])
    c_sc = dt_in("c_sc", [128, 1024])

    y_p = dt_out("y_p", [NT, DM])
    y_s = dt_out("y_s", [TS, DM])
    kvp = dt_out("kvp", [2, 2048, 1024])
    kvs = dt_out("kvs", [2, TS, 1024])
    glap = dt_out("glap", [2, 128, 64])
    glas = dt_out("glas", [2, NSQ, 128, 64])
    retp = dt_out("retp", [2, 128, 64])
    rets = dt_out("rets", [2, NSQ, 128, 64])

    Z = nc.dram_tensor("Zs", [NR, 1536], F32)
    OBC = nc.dram_tensor("OBCs", [NR, 512], F32)
    OA = nc.dram_tensor("OAs", [3, NT, 520], F32)
    OAS = nc.dram_tensor("OASs", [128, 512], F32)
    XR = nc.dram_tensor("XRs", [NR, DM], F32)
    UVB = [nc.dram_tensor("UVB0", [16384, 2048], BF16), nc.dram_tensor("UVB1", [16384, 2048], BF16)]

    S = Sched(nc)
    es = ExitStack()

    def sb(name, shape, dt=F32):
        return es.enter_context(nc.sbuf_tensor(name, list(shape), dt))

    def V(fn, r=(), w=()):
        S.op("dve", fn, [t.r for t in r], [t.r for t in w])

    def A(fn, r=(), w=()):
        S.op("act", fn, [t.r for t in r], [t.r for t in w])

    def G(fn, r=(), w=()):
        S.op("pool", fn, [t.r for t in r], [t.r for t in w])

    def P(fn, r=(), w=()):
        S.op("pe", fn, [t.r for t in r], [t.r for t in w])

    def D(q, out, in_, r=(), w=()):
        S.dma(q, lambda e: e.dma_start(out=out, in_=in_), [t.r for t in r], [t.r for t in w])

    def mm(out, lhsT, rhs, start, stop, r, w):
        P(lambda e: e.matmul(out, lhsT=lhsT, rhs=rhs, start=start, stop=stop), r, w)

    def tr(out, in_, ident, r, w):
        P(lambda e: e.transpose(out=out, in_=in_, identity=ident), r, w)

    wbig = T(sb("wbig", [128, 8 * DIN], BF16), "wbig")
    z = T(sb("z", [128, DIN], F32), "z")
    cf = T(sb("cf", [128, NCF], F32), "cf")
    identb = T(sb("identb", [128, 128], BF16), "identb")
    onesb = T(sb("onesb", [1, 128], BF16), "onesb")
    Sst = [T(sb("Sg", [128, 64], F32), "Sg"), T(sb("Sr", [128, 64], F32), "Sr")]
    arena_f = sb("arena_f", [128, 24576], F32)
    arena_b = sb("arena_b", [128, 16896], BF16)
    pf = [T(es.enter_context(nc.psum_tensor("pf%d" % i, [128, 512], F32)), "pf%d" % i) for i in range(6)]
    pb = [T(es.enter_context(nc.psum_tensor("pb%d" % i, [128, 1024], BF16)), "pb%d" % i) for i in range(2)]
    cnt = {"f": 0, "b": 0, "af": 0, "ab": 0, "ev": 0, "dq": 0}

    def nf():
        cnt["f"] += 1
        return pf[cnt["f"] % 6]

    def nbk():
        cnt["b"] += 1
        return pb[cnt["b"] % 2]

    def reset_arena():
        cnt["af"] = 0
        cnt["ab"] = 0

    def cf32(n, name=""):
        o = cnt["af"]
        cnt["af"] = o + n
        assert cnt["af"] <= 24576, ("arena_f overflow", name, cnt["af"])
        return T(arena_f[:, o:o + n], name)

    def cb16(n, name=""):
        o = cnt["ab"]
        n2 = (n + 1) // 2 * 2
        cnt["ab"] = o + n2
        assert cnt["ab"] <= 16896, ("arena_b overflow", name, cnt["ab"])
        return T(arena_b[:, o:o + n], name)

    def evac(out, in_, r, w):
        cnt["ev"] += 1
        if cnt["ev"] % 2:
            A(lambda e: e.activation(out=out, in_=in_, func=AF.Copy), r, w)
        else:
            V(lambda e: e.tensor_copy(out=out, in_=in_), r, w)

    dummy = lambda: T(None, "dummy")

    D("sp", cf.a[:], c_f32.ap(), w=[cf])
    V(lambda e: e.tensor_copy(out=identb.a[:], in_=cf.a[:, C_ID:C_ID + 128]), [cf], [identb])
    V(lambda e: e.tensor_copy(out=onesb.a[:], in_=cf.a[0:1, C_ONE:C_ONE + 128]), [cf], [onesb])
    identf = cf.a[:, C_ID:C_ID + 128]
    HM = cf.a[:, C_HM:C_HM + 4]
    HALF = cf.a[:, C_HALF:C_HALF + 2]

    def load_weight(dram3, l, ncols, dst_view):
        for kc in range(8):
            D("sp", z.a[:, 0:ncols], dram3.ap()[l, kc * 128:(kc + 1) * 128, :], w=[z])
            G(lambda e, kc=kc: e.tensor_copy(out=dst_view[:, kc, :], in_=z.a[:, 0:ncols]), [z], [wbig])

    def rmsnorm(x_t, g_t, outs, junk, ss):
        A(lambda e: e.activation(out=junk.a[:], in_=x_t.a[:], func=AF.Square, accum_out=ss.a[:, 0:1]), [x_t], [junk, ss])
        V(lambda e: e.tensor_scalar(out=ss.a[:], in0=ss.a[:], scalar1=1.0 / DM, scalar2=EPS, op0=ALU.mult, op1=ALU.add), [ss], [ss])
        A(lambda e: e.activation(out=ss.a[:], in_=ss.a[:], func=AF.Sqrt), [ss], [ss])
        V(lambda e: e.reciprocal(out=ss.a[:], in_=ss.a[:]), [ss], [ss])
        for (t, ap) in outs:
            V(lambda e, ap=ap: e.scalar_tensor_tensor(out=ap, in0=x_t.a[:], scalar=ss.a[:, 0:1], in1=g_t.a[:],
                                                       op0=ALU.mult, op1=ALU.mult), [x_t, ss, g_t], [t])

    def transpose8(src_b, dstT, n=8):
        p = nbk()
        for kc in range(n):
            tr(p.a[:, kc * 128:(kc + 1) * 128], src_b.a[:, kc * 128:(kc + 1) * 128], identb.a[:], [src_b, identb], [p])
        evac(dstT.a[:, 0:n * 128], p.a[:, 0:n * 128], [p], [dstT])

    reset_arena()
    cst = [cf32(2048, "cst%d" % i) for i in range(4)]
    cbt = [cb16(2048, "cbt%d" % i) for i in range(4)]
    ci_ = 0
    for l in range(2):
        for i in range(128):
            k4 = ci_ % 4
            D("sp" if ci_ % 2 == 0 else "act", cst[k4].a[:], uvs[l].ap()[i * 128:(i + 1) * 128, :], w=[cst[k4]])
            if ci_ % 3 == 0:
                V(lambda e, k4=k4: e.tensor_copy(out=cbt[k4].a[:], in_=cst[k4].a[:]), [cst[k4]], [cbt[k4]])
            elif ci_ % 3 == 1:
                G(lambda e, k4=k4: e.tensor_copy(out=cbt[k4].a[:], in_=cst[k4].a[:]), [cst[k4]], [cbt[k4]])
            else:
                A(lambda e, k4=k4: e.activation(out=cbt[k4].a[:], in_=cst[k4].a[:], func=AF.Copy), [cst[k4]], [cbt[k4]])
            D("pool", UVB[l].ap()[i * 128:(i + 1) * 128, :], cbt[k4].a[:], r=[cbt[k4]], w=[dummy()])
            ci_ += 1
    for l in range(2):
        S.barrier()
        reset_arena()
        w_in_v = wbig.a[:, :].rearrange("p (k c) -> p k c", k=8)
        load_weight(w_in, l, DIN, w_in_v)
        gmix = cf32(1024, "gmix")
        D("sp", gmix.a[:], g_mix.ap()[l].partition_broadcast(128), w=[gmix])
        ggla = cf32(64, "ggla")
        D("sp", ggla.a[:], g_gla.ap()[l].partition_broadcast(128), w=[ggla])
        wg2f = cf32(128, "wg2f")
        D("sp", wg2f.a[0:16, :], w_g2.ap()[l], w=[wg2f])
        bgf = cf32(128, "bgf")
        D("sp", bgf.a[0:1, :], b_g.ap()[l], w=[bgf])
        wg2 = cb16(128, "wg2")
        bgb = cb16(128, "bgb")
        V(lambda e: e.tensor_copy(out=wg2.a[0:16, :], in_=wg2f.a[0:16, :]), [wg2f], [wg2])
        V(lambda e: e.tensor_copy(out=bgb.a[0:1, :], in_=bgf.a[0:1, :]), [bgf], [bgb])
        S0 = [cf32(1024, "S0g"), cf32(1024, "S0r")]
        D("sp", S0[0].a[:, :].rearrange("p (s e) -> p s e", s=NSQ), sgla.ap()[l].rearrange("s p e -> p s e"), w=[S0[0]])
        D("sp", S0[1].a[:, :].rearrange("p (s e) -> p s e", s=NSQ), sret.ap()[l].rearrange("s p e -> p s e"), w=[S0[1]])
        Sfin = [cf32(1024, "Sfg"), cf32(1024, "Sfr")]
        seqcol = cb16(1024, "seqcol")
        scst = cf32(1024, "scst")
        D("sp", scst.a[:], c_sc.ap(), w=[scst])
        V(lambda e: e.tensor_copy(out=seqcol.a[:], in_=scst.a[:]), [scst], [seqcol])
        for m in range(2):
            V(lambda e, m=m: e.memset(Sst[m].a[:], 0.0), [], [Sst[m]])
        xt = cf32(1024, "x")
        junkA = cf32(1024, "junkA")
        ssA = cf32(1, "ssA")
        hnb = cb16(1024, "hnb")
        hnT = cb16(1024, "hnT")
        la = cf32(256, "la")
        V(lambda e: e.tensor_copy(out=la.a[:, 128:256], in_=cf.a[:, C_LGAM:C_LGAM + 128]), [cf], [la])
        gbb = cb16(16, "gbb")
        gbT = cb16(128, "gbT")
        e1 = cf32(128, "e1")
        bs = cf32(256, "bs")
        eb = cf32(256, "eb")
        enb = cf32(256, "enb")
        ebl = cf32(256, "ebl")
        dec = cf32(32, "dec")
        qtb = cb16(128, "qtb")
        ktb = cb16(128, "ktb")
        khb = cb16(128, "khb")
        khm = cb16(128, "khm")
        vb16 = cb16(256, "vb16")
        qkT = cb16(256, "qkT")
        qTm = cb16(512, "qTm")
        qTcm = cb16(1024, "qTcm")
        attm = cb16(512, "attm")
        Sbd = [cb16(256, "Sbdg"), cb16(256, "Sbdr")]
        Sbds = [cb16(NSQ * 256, "Sbdsg"), cb16(NSQ * 256, "Sbdsr")]
        o1 = cf32(256, "o1")
        om = cf32(256, "om")
        tmp = cf32(256, "tmp")
        kvsel = cf32(64, "kvsel")
        st4 = cf32(8, "st4")
        sil = cf32(256, "sil")
        obc = cf32(512, "obc")
        for m in range(2):
            V(lambda e, m=m: e.tensor_tensor(
                out=Sbds[m].a[:, :].rearrange("p (s h e) -> p s h e", s=NSQ, h=4),
                in0=S0[m].a[:, :].rearrange("p (s e) -> p s e", s=NSQ).unsqueeze(2).to_broadcast([128, NSQ, 4, 64]),
                in1=HM.unsqueeze(1).unsqueeze(3).to_broadcast([128, NSQ, 4, 64]), op=ALU.mult), [S0[m], cf], [Sbds[m]])

        for b in range(NB + 1):
            samp = (b == NB)
            Tn = TS if samp else 128
            nseq = NSQ if samp else 1
            if l == 0:
                src = xs.ap() if samp else xp.ap()[b * 128:(b + 1) * 128, :]
            else:
                src = XR.ap()[b * 128:(b + 1) * 128, :]
            D("sp", xt.a[:], src, w=[xt])
            rmsnorm(xt, gmix, [(hnb, hnb.a[:])], junkA, ssA)
            transpose8(hnb, hnT)
            hnTv = hnT.a[:, :].rearrange("p (k t) -> p k t", k=8)
            for n in range(7):
                c0 = n * 512
                cw = min(512, DIN - c0)
                p = nf()
                for kc in range(8):
                    mm(p.a[:, 0:cw], hnTv[:, kc, :], w_in_v[:, kc, c0:c0 + cw], kc == 0, kc == 7, [hnT, wbig], [p])
                evac(z.a[:, c0:c0 + cw], p.a[:, 0:cw], [p], [z])
            D("sp", Z.ap()[b * 128:(b + 1) * 128, :], z.a[:, 0:1536], r=[z], w=[dummy()])
            if samp:
                D("sp", kvs.ap()[l], z.a[0:TS, 512:1536], r=[z], w=[dummy()])
            elif b >= KVB0:
                D("sp", kvp.ap()[l, (b - KVB0) * 128:(b - KVB0 + 1) * 128, :], z.a[:, 512:1536], r=[z], w=[dummy()])

            CUM = cf.a[:, C_SCUM:C_SCUM + 128] if samp else cf.a[:, C_TRI:C_TRI + 128]
            TOT = cf.a[:, C_SSAME:C_SSAME + 128] if samp else cf.a[:, C_ONE:C_ONE + 128]
            SEQI = cf.a[:, C_SEQI:C_SEQI + 16] if samp else cf.a[:, C_ONEC:C_ONEC + 1]
            V(lambda e: e.tensor_copy(out=gbb.a[:Tn, :], in_=z.a[:Tn, 2304:2320]), [z], [gbb])
            p = nbk()
            tr(p.a[0:16, 0:Tn], gbb.a[:Tn, 0:16], identb.a[:Tn, :Tn], [gbb, identb], [p])
            evac(gbT.a[0:16, 0:Tn], p.a[0:16, 0:Tn], [p], [gbT])
            pg = nf()
            mm(pg.a[:Tn, 0:128], gbT.a[0:16, 0:Tn], wg2.a[0:16, :], True, False, [gbT, wg2], [pg])
            mm(pg.a[:Tn, 0:128], onesb.a[0:1, 0:Tn], bgb.a[0:1, :], False, True, [onesb, bgb], [pg])
            A(lambda e, pg=pg: e.activation(out=e1.a[:Tn, :], in_=pg.a[:Tn, 0:128], func=AF.Exp, scale=-1.0), [pg], [e1])
            A(lambda e: e.activation(out=e1.a[:Tn, :], in_=e1.a[:Tn, :], func=AF.Ln, bias=1.0), [e1], [e1])
            A(lambda e: e.activation(out=la.a[:Tn, 0:128], in_=e1.a[:Tn, :], func=AF.Copy, scale=-1.0 / 16.0), [e1], [la])
            pc = nf()
            mm(pc.a[:Tn, 0:256], CUM[:Tn, :Tn], la.a[:Tn, :], True, True, [cf, la], [pc])
            mm(pc.a[:Tn, 256:512], TOT[:Tn, :Tn], la.a[:Tn, :], True, True, [cf, la], [pc])
            pd = nf()
            mm(pd.a[:, 0:nseq], la.a[:Tn, 0:128], SEQI[:Tn, 0:nseq], True, True, [la, cf], [pd])
            mm(pd.a[:, 16:16 + nseq], la.a[:Tn, 128:256], SEQI[:Tn, 0:nseq], True, True, [la, cf], [pd])
            A(lambda e, pd=pd: e.activation(out=dec.a[:, :], in_=pd.a[:, 0:32], func=AF.Exp), [pd], [dec])
            A(lambda e, pc=pc: e.activation(out=bs.a[:Tn, :], in_=pc.a[:Tn, 0:256], func=AF.Copy), [pc], [bs])
            A(lambda e: e.activation(out=eb.a[:Tn, :], in_=bs.a[:Tn, :], func=AF.Exp), [bs], [eb])
            A(lambda e: e.activation(out=enb.a[:Tn, :], in_=bs.a[:Tn, :], func=AF.Exp, scale=-1.0), [bs], [enb])
            V(lambda e, pc=pc: e.tensor_tensor(out=ebl.a[:Tn, :], in0=pc.a[:Tn, 256:512], in1=bs.a[:Tn, :], op=ALU.subtract), [pc, bs], [ebl])
            A(lambda e: e.activation(out=ebl.a[:Tn, :], in_=ebl.a[:Tn, :], func=AF.Exp), [ebl], [ebl])
            for m in range(2):
                qc0 = 1536 if m == 0 else 2320
                kc0 = qc0 + 128
                vc0 = qc0 + 256
                gc0 = qc0 + 512 if m == 0 else qc0 + 512
                gate0 = 2048 if m == 0 else 2832
                ms = slice(m * 128, (m + 1) * 128)
                V(lambda e, qc0=qc0, ms=ms: e.scalar_tensor_tensor(out=qtb.a[:Tn, :], in0=z.a[:Tn, qc0:qc0 + 128], scalar=32.0 ** -0.5,
                                                                    in1=eb.a[:Tn, ms], op0=ALU.mult, op1=ALU.mult), [z, eb], [qtb])
                V(lambda e, kc0=kc0, ms=ms: e.tensor_tensor(out=ktb.a[:Tn, :], in0=z.a[:Tn, kc0:kc0 + 128], in1=enb.a[:Tn, ms], op=ALU.mult), [z, enb], [ktb])
                V(lambda e, kc0=kc0, ms=ms: e.tensor_tensor(out=khb.a[:Tn, :], in0=z.a[:Tn, kc0:kc0 + 128], in1=ebl.a[:Tn, ms], op=ALU.mult), [z, ebl], [khb])
                G(lambda e, vc0=vc0: e.tensor_copy(out=vb16.a[:Tn, :], in_=z.a[:Tn, vc0:vc0 + 256]), [z], [vb16])
                p = nbk()
                tr(p.a[:, 0:Tn], qtb.a[:Tn, :], identb.a[:Tn, :Tn], [qtb, identb], [p])
                tr(p.a[:, 128:128 + Tn], ktb.a[:Tn, :], identb.a[:Tn, :Tn], [ktb, identb], [p])
                evac(qkT.a[:, 0:256], p.a[:, 0:256], [p], [qkT])
                V(lambda e: e.tensor_tensor(out=qTm.a[:, :].rearrange("p (h t) -> p h t", h=4)[:, :, 0:Tn],
                                            in0=qkT.a[:, 0:Tn].unsqueeze(1).to_broadcast([128, 4, Tn]),
                                            in1=HM.unsqueeze(2).to_broadcast([128, 4, Tn]), op=ALU.mult), [qkT, cf], [qTm])
                pa = nf()
                for h in range(4):
                    mm(pa.a[:Tn, h * 128:h * 128 + Tn], qkT.a[:, 128:128 + Tn], qTm.a[:, h * 128:h * 128 + Tn], True, True, [qkT, qTm], [pa])
                V(lambda e, pa=pa: e.tensor_tensor(out=attm.a[:Tn, :].rearrange("p (h t) -> p h t", h=4)[:, :, 0:Tn],
                                                   in0=pa.a[:Tn, :].rearrange("p (h t) -> p h t", h=4)[:, :, 0:Tn],
                                                   in1=CUM[:Tn, :Tn].unsqueeze(1).to_broadcast([Tn, 4, Tn]), op=ALU.mult), [pa, cf], [attm])
                po = nf()
                for h in range(4):
                    mm(po.a[:Tn, h * 64:(h + 1) * 64], attm.a[:Tn, h * 128:h * 128 + Tn], vb16.a[:Tn, h * 64:(h + 1) * 64], True, True, [attm, vb16], [po])
                if not samp:
                    V(lambda e, m=m: e.tensor_tensor(out=Sbd[m].a[:, :].rearrange("p (h e) -> p h e", h=4),
                                                     in0=Sst[m].a[:, :].unsqueeze(1).to_broadcast([128, 4, 64]),
                                                     in1=HM.unsqueeze(2).to_broadcast([128, 4, 64]), op=ALU.mult), [Sst[m], cf], [Sbd[m]])
                    mm(po.a[:Tn, 256:512], qkT.a[:, 0:Tn], Sbd[m].a[:, :], True, True, [qkT, Sbd[m]], [po])
                else:
                    V(lambda e: e.tensor_tensor(out=qTcm.a[:, :].rearrange("p (s t) -> p s t", s=NSQ),
                                                in0=qkT.a[:, 0:TS].unsqueeze(1).to_broadcast([128, NSQ, TS]),
                                                in1=seqcol.a[:, :].rearrange("p (s t) -> p s t", s=NSQ), op=ALU.mult), [qkT, seqcol], [qTcm])
                    for s in range(NSQ):
                        mm(po.a[:Tn, 256:512], qTcm.a[:, s * TS:(s + 1) * TS], Sbds[m].a[:, s * 256:(s + 1) * 256], s == 0, s == NSQ - 1, [qTcm, Sbds[m]], [po])
                A(lambda e, po=po: e.activation(out=o1.a[:Tn, :], in_=po.a[:Tn, 0:256], func=AF.Copy), [po], [o1])
                V(lambda e, po=po: e.tensor_tensor(out=om.a[:Tn, :], in0=po.a[:Tn, 256:512], in1=o1.a[:Tn, :], op=ALU.add), [po, o1], [om])
                if not samp:
                    pk = nf()
                    mm(pk.a[:, 0:256], khb.a[:Tn, :], vb16.a[:Tn, :], True, True, [khb, vb16], [pk])
                    V(lambda e, pk=pk: e.tensor_tensor(out=tmp.a[:, :].rearrange("p (h e) -> p h e", h=4),
                                                       in0=pk.a[:, 0:256].rearrange("p (h e) -> p h e", h=4),
                                                       in1=HM.unsqueeze(2).to_broadcast([128, 4, 64]), op=ALU.mult), [pk, cf], [tmp])
                    V(lambda e: e.tensor_reduce(out=kvsel.a[:, :], in_=tmp.a[:, :].rearrange("p (h e) -> p e h", h=4), axis=AX.X, op=ALU.add), [tmp], [kvsel])
                    V(lambda e, m=m: e.scalar_tensor_tensor(out=Sst[m].a[:, :], in0=Sst[m].a[:, :], scalar=dec.a[:, m * 16:m * 16 + 1],
                                                            in1=kvsel.a[:, :], op0=ALU.mult, op1=ALU.add), [Sst[m], dec, kvsel], [Sst[m]])
                    if b == NB - 1:
                        D("sp", (glap if m == 0 else retp).ap()[l], Sst[m].a[:, :], r=[Sst[m]], w=[dummy()])
                else:
                    for s in range(NSQ):
                        V(lambda e, s=s: e.tensor_scalar(out=khm.a[:Tn, :], in0=khb.a[:Tn, :], scalar1=SEQI[:Tn, s:s + 1], scalar2=None, op0=ALU.mult), [khb, cf], [khm])
                        pk = nf()
                        mm(pk.a[:, 0:256], khm.a[:Tn, :], vb16.a[:Tn, :], True, True, [khm, vb16], [pk])
                        V(lambda e, pk=pk: e.tensor_tensor(out=tmp.a[:, :].rearrange("p (h e) -> p h e", h=4),
                                                           in0=pk.a[:, 0:256].rearrange("p (h e) -> p h e", h=4),
                                                           in1=HM.unsqueeze(2).to_broadcast([128, 4, 64]), op=ALU.mult), [pk, cf], [tmp])
                        V(lambda e: e.tensor_reduce(out=kvsel.a[:, :], in_=tmp.a[:, :].rearrange("p (h e) -> p e h", h=4), axis=AX.X, op=ALU.add), [tmp], [kvsel])
                        V(lambda e, m=m, s=s: e.scalar_tensor_tensor(out=Sfin[m].a[:, s * 64:(s + 1) * 64], in0=S0[m].a[:, s * 64:(s + 1) * 64],
                                                                    scalar=dec.a[:, m * 16 + s:m * 16 + s + 1], in1=kvsel.a[:, :],
                                                                    op0=ALU.mult, op1=ALU.add), [S0[m], dec, kvsel], [Sfin[m]])
                    D("sp", (glas if m == 0 else rets).ap()[l].rearrange("s p e -> p s e"),
                      Sfin[m].a[:, :].rearrange("p (s e) -> p s e", s=NSQ), r=[Sfin[m]], w=[dummy()])
                om3 = om.a[:Tn, :].rearrange("p (h e) -> p h e", h=4)
                tmp3 = tmp.a[:Tn, :].rearrange("p (h e) -> p h e", h=4)
                if m == 1:
                    V(lambda e: e.tensor_reduce(out=st4.a[:Tn, 0:4], in_=om3, axis=AX.X, op=ALU.add), [om], [st4])
                    V(lambda e: e.tensor_scalar(out=st4.a[:Tn, 0:4], in0=st4.a[:Tn, 0:4], scalar1=-1.0 / 64.0, scalar2=None, op0=ALU.mult), [st4], [st4])
                    V(lambda e: e.tensor_tensor(out=om3, in0=om3, in1=st4.a[:Tn, 0:4].unsqueeze(2).to_broadcast([Tn, 4, 64]), op=ALU.add), [om, st4], [om])
                V(lambda e: e.tensor_tensor(out=tmp.a[:Tn, :], in0=om.a[:Tn, :], in1=om.a[:Tn, :], op=ALU.mult), [om], [tmp])
                V(lambda e: e.tensor_reduce(out=st4.a[:Tn, 4:8], in_=tmp3, axis=AX.X, op=ALU.add), [tmp], [st4])
                V(lambda e: e.tensor_scalar(out=st4.a[:Tn, 4:8], in0=st4.a[:Tn, 4:8], scalar1=1.0 / 64.0, scalar2=EPS, op0=ALU.mult, op1=ALU.add), [st4], [st4])
                A(lambda e: e.activation(out=st4.a[:Tn, 4:8], in_=st4.a[:Tn, 4:8], func=AF.Sqrt), [st4], [st4])
                V(lambda e: e.reciprocal(out=st4.a[:Tn, 4:8], in_=st4.a[:Tn, 4:8]), [st4], [st4])
                V(lambda e: e.tensor_tensor(out=om3, in0=om3, in1=st4.a[:Tn, 4:8].unsqueeze(2).to_broadcast([Tn, 4, 64]), op=ALU.mult), [om, st4], [om])
                if m == 0:
                    V(lambda e: e.tensor_tensor(out=om3, in0=om3, in1=ggla.a[:Tn, :].unsqueeze(1).to_broadcast([Tn, 4, 64]), op=ALU.mult), [om, ggla], [om])
                A(lambda e, gate0=gate0: e.activation(out=sil.a[:Tn, :], in_=z.a[:Tn, gate0:gate0 + 256], func=AF.Silu), [z], [sil])
                V(lambda e, m=m: e.tensor_tensor(out=obc.a[:Tn, m * 256:(m + 1) * 256], in0=om.a[:Tn, :], in1=sil.a[:Tn, :], op=ALU.mult), [om, sil], [obc])
            D("sp", OBC.ap()[b * 128:b * 128 + Tn, :], obc.a[:Tn, :], r=[obc], w=[dummy()])

        S.barrier()
        reset_arena()
        BIAS = cb16(48 * 128, "BIAS")
        bst = cf32(2048, "bst")
        for i in range(3):
            D("sp", bst.a[:], c_bias.ap()[:, i * 2048:(i + 1) * 2048], w=[bst])
            V(lambda e, i=i: e.tensor_copy(out=BIAS.a[:, i * 2048:(i + 1) * 2048], in_=bst.a[:]), [bst], [BIAS])
        Qf = [cf32(512, "Qf0"), cf32(512, "Qf1")]
        KVf = [cf32(1024, "KVf0"), cf32(1024, "KVf1")]
        Qb = cb16(512, "Qb")
        Kb = cb16(512, "Kb")
        QTm = cb16(1024, "QTm")
        KT = [cb16(512, "KT0"), cb16(512, "KT1")]
        Vt = [cb16(8 * 65 + 8, "V0"), cb16(8 * 65 + 8, "V1")]
        PT = [cb16(512, "PT0"), cb16(512, "PT1")]
        OAt = [cf32(520, "OAt0"), cf32(520, "OAt1")]
        for i in range(2):
            G(lambda e, i=i: e.memset(Vt[i].a[:, 0:520], 1.0), [], [Vt[i]])
        tcount = 0
        for g, d in enumerate(DILS):
            nblk = NT // (d * 128)
            for r_ in range(d):
                for n in range(nblk):
                    base = r_ + d * 128 * n
                    qf = Qf[tcount % 2]
                    kvf = KVf[tcount % 2]
                    oat = OAt[tcount % 2]
                    cur = n % 2
                    prv = 1 - cur
                    tcount += 1
                    D("sp", qf.a[:], bass.AP(Z, base * 1536, [[d * 1536, 128], [1, 512]]), w=[qf])
                    D("sp", kvf.a[:], bass.AP(Z, base * 1536 + 512, [[d * 1536, 128], [1, 1024]]), w=[kvf])
                    A(lambda e, qf=qf: e.activation(out=Qb.a[:], in_=qf.a[:], func=AF.Copy, scale=0.125), [qf], [Qb])
                    G(lambda e, kvf=kvf: e.tensor_copy(out=Kb.a[:], in_=kvf.a[:, 0:512]), [kvf], [Kb])
                    G(lambda e, kvf=kvf, cur=cur: e.tensor_copy(out=Vt[cur].a[:, 0:520].rearrange("p (h e) -> p h e", h=8)[:, :, 0:64],
                                                                 in_=kvf.a[:, 512:1024].rearrange("p (h e) -> p h e", h=8)), [kvf], [Vt[cur]])
                    p = nbk()
                    for hp in range(4):
                        tr(p.a[:, hp * 128:(hp + 1) * 128], Qb.a[:, hp * 128:(hp + 1) * 128], identb.a[:], [Qb, identb], [p])
                    for e_ in range(2):
                        V(lambda e, p=p, e_=e_: e.tensor_scalar(out=QTm.a[:, :].rearrange("p (a b t) -> p a b t", a=4, b=2)[:, :, e_, :],
                                                               in0=p.a[:, 0:512].rearrange("p (a t) -> p a t", a=4),
                                                               scalar1=HALF[:, e_:e_ + 1], scalar2=None, op0=ALU.mult), [p, cf], [QTm])
                    p2 = nbk()
                    for hp in range(4):
                        tr(p2.a[:, hp * 128:(hp + 1) * 128], Kb.a[:, hp * 128:(hp + 1) * 128], identb.a[:], [Kb, identb], [p2])
                    A(lambda e, p2=p2, cur=cur: e.activation(out=KT[cur].a[:], in_=p2.a[:, 0:512], func=AF.Copy), [p2], [KT[cur]])
                    poA = nf()
                    poB = nf()
                    for hp in range(4):
                        ps = nf()
                        pt = PT[hp % 2]
                        for hh in range(2):
                            h = 2 * hp + hh
                            c0 = hh * 256
                            if n > 0:
                                mm(ps.a[:, c0:c0 + 128], KT[prv].a[:, hp * 128:(hp + 1) * 128], QTm.a[:, h * 128:(h + 1) * 128], True, False, [KT[prv], QTm], [ps])
                                bo = ((g * 2 + 0) * 8 + h) * 128
                                mm(ps.a[:, c0:c0 + 128], identb.a[:], BIAS.a[:, bo:bo + 128], False, True, [identb, BIAS], [ps])
                            mm(ps.a[:, c0 + 128:c0 + 256], KT[cur].a[:, hp * 128:(hp + 1) * 128], QTm.a[:, h * 128:(h + 1) * 128], True, False, [KT[cur], QTm], [ps])
                            bo = ((g * 2 + 1) * 8 + h) * 128
                            mm(ps.a[:, c0 + 128:c0 + 256], identb.a[:], BIAS.a[:, bo:bo + 128], False, True, [identb, BIAS], [ps])
                        if n > 0:
                            A(lambda e, ps=ps, pt=pt: e.activation(out=pt.a[:], in_=ps.a[:], func=AF.Exp), [ps], [pt])
                        else:
                            A(lambda e, ps=ps, pt=pt: e.activation(out=pt.a[:, :].rearrange("p (a c) -> p a c", a=2)[:, :, 128:256],
                                                                   in_=ps.a[:, :].rearrange("p (a c) -> p a c", a=2)[:, :, 128:256], func=AF.Exp), [ps], [pt])
                        for hh in range(2):
                            h = 2 * hp + hh
                            po = poA if h < 4 else poB
                            c = (h % 4) * 65
                            c0 = hh * 256
                            if n > 0:
                                mm(po.a[:, c:c + 65], pt.a[:, c0:c0 + 128], Vt[prv].a[:, h * 65:(h + 1) * 65], True, False, [pt, Vt[prv]], [po])
                                mm(po.a[:, c:c + 65], pt.a[:, c0 + 128:c0 + 256], Vt[cur].a[:, h * 65:(h + 1) * 65], False, True, [pt, Vt[cur]], [po])
                            else:
                                mm(po.a[:, c:c + 65], pt.a[:, c0 + 128:c0 + 256], Vt[cur].a[:, h * 65:(h + 1) * 65], True, True, [pt, Vt[cur]], [po])
                    A(lambda e, poA=poA, oat=oat: e.activation(out=oat.a[:, 0:260], in_=poA.a[:, 0:260], func=AF.Copy), [poA], [oat])
                    V(lambda e, poB=poB, oat=oat: e.tensor_copy(out=oat.a[:, 260:520], in_=poB.a[:, 0:260]), [poB], [oat])
                    D("pool", bass.AP(OA, (g * NT + base) * 520, [[d * 520, 128], [1, 520]]), oat.a[:], r=[oat], w=[dummy()])

        S.barrier()
        reset_arena()
        SBt = cb16(288, "SB")
        NBt = cb16(1024, "NB")
        bst = cf32(1024, "bst")
        D("sp", bst.a[:, 0:288], c_sb.ap(), w=[bst])
        V(lambda e: e.tensor_copy(out=SBt.a[:], in_=bst.a[:, 0:288]), [bst], [SBt])
        D("sp", bst.a[:], c_nb.ap(), w=[bst])
        V(lambda e: e.tensor_copy(out=NBt.a[:], in_=bst.a[:]), [bst], [NBt])
        Qf = cf32(512, "Qfs")
        KVf = [cf32(1024, "KVfs0"), cf32(1024, "KVfs1")]
        Qb = cb16(512, "Qbs")
        Kb = cb16(512, "Kbs")
        QTm = cb16(8 * 64, "QTms")
        KT = cb16(512, "KTs")
        Vt = cb16(528, "Vs")
        PTn = cb16(512, "PTn")
        PTs = cb16(NSQ * 512, "PTs")
        acc = cf32(520, "acc")
        rden = cf32(8, "rden")
        oas = cf32(512, "oas")
        G(lambda e: e.memset(Vt.a[:, 0:520], 1.0), [], [Vt])
        G(lambda e: e.memset(PTs.a[:], 0.0), [], [PTs])
        V(lambda e: e.memset(acc.a[:], 0.0), [], [acc])
        D("sp", Qf.a[0:TS, :], Z.ap()[NT:NT + TS, 0:512], w=[Qf])
        D("sp", KVf[0].a[0:TS, :], Z.ap()[NT:NT + TS, 512:1536], w=[KVf[0]])
        A(lambda e: e.activation(out=Qb.a[0:TS, :], in_=Qf.a[0:TS, :], func=AF.Copy, scale=0.125), [Qf], [Qb])
        G(lambda e: e.tensor_copy(out=Kb.a[0:TS, :], in_=KVf[0].a[0:TS, 0:512]), [KVf[0]], [Kb])
        G(lambda e: e.tensor_copy(out=Vt.a[0:TS, 0:520].rearrange("p (h e) -> p h e", h=8)[:, :, 0:64],
                                  in_=KVf[0].a[0:TS, 512:1024].rearrange("p (h e) -> p h e", h=8)), [KVf[0]], [Vt])
        p = nbk()
        for hp in range(4):
            tr(p.a[:, hp * 64:(hp + 1) * 64], Qb.a[0:TS, hp * 128:(hp + 1) * 128], identb.a[:TS, :TS], [Qb, identb], [p])
        for e_ in range(2):
            V(lambda e, p=p, e_=e_: e.tensor_scalar(out=QTm.a[:, :].rearrange("p (a b t) -> p a b t", a=4, b=2)[:, :, e_, :],
                                                   in0=p.a[:, 0:256].rearrange("p (a t) -> p a t", a=4),
                                                   scalar1=HALF[:, e_:e_ + 1], scalar2=None, op0=ALU.mult), [p, cf], [QTm])
        p2 = nbk()
        for hp in range(4):
            tr(p2.a[:, hp * 64:(hp + 1) * 64], Kb.a[0:TS, hp * 128:(hp + 1) * 128], identb.a[:TS, :TS], [Kb, identb], [p2])
        A(lambda e, p2=p2: e.activation(out=KT.a[:, 0:256], in_=p2.a[:, 0:256], func=AF.Copy), [p2], [KT])
        for ps_i in range(2):
            ps = nf()
            for h in range(8):
                mm(ps.a[:TS, h * 64:(h + 1) * 64], KT.a[:, (h // 2) * 64:(h // 2 + 1) * 64], QTm.a[:, h * 64:(h + 1) * 64], True, False, [KT, QTm], [ps])
                bo = (ps_i * 8 + h) * 64
                mm(ps.a[:TS, h * 64:(h + 1) * 64], identb.a[:TS, :TS], NBt.a[:TS, bo:bo + 64], False, True, [identb, NBt], [ps])
            A(lambda e, ps=ps: e.activation(out=PTn.a[:TS, :], in_=ps.a[:TS, :], func=AF.Exp), [ps], [PTn])
            poA = nf()
            poB = nf()
            for h in range(8):
                po = poA if h < 4 else poB
                c = (h % 4) * 65
                mm(po.a[:TS, c:c + 65], PTn.a[:TS, h * 64:(h + 1) * 64], Vt.a[:TS, h * 65:(h + 1) * 65], True, True, [PTn, Vt], [po])
            wgt = 1.0 if ps_i == 0 else 2.0
            V(lambda e, poA=poA, wgt=wgt: e.scalar_tensor_tensor(out=acc.a[:TS, 0:260], in0=poA.a[:TS, 0:260], scalar=wgt, in1=acc.a[:TS, 0:260],
                                                                 op0=ALU.mult, op1=ALU.add), [poA, acc], [acc])
            V(lambda e, poB=poB, wgt=wgt: e.scalar_tensor_tensor(out=acc.a[:TS, 260:520], in0=poB.a[:TS, 0:260], scalar=wgt, in1=acc.a[:TS, 260:520],
                                                                 op0=ALU.mult, op1=ALU.add), [poB, acc], [acc])
        Vc = [cb16(528, "Vc0"), cb16(528, "Vc1")]
        KTc = [cb16(512, "KTc0"), cb16(512, "KTc1")]
        Kbc = [cb16(512, "Kbc0"), cb16(512, "Kbc1")]
        for i in range(2):
            G(lambda e, i=i: e.memset(Vc[i].a[:, 0:520], 1.0), [], [Vc[i]])
        tcount = 0
        for s in range(NSQ):
            for ty in range(9):
                if ty == 0:
                    row0, st_ = 1920, 1
                elif ty <= 4:
                    row0, st_ = 1536 + (ty - 1), 4
                else:
                    row0, st_ = (ty - 5), 16
                i2 = tcount % 2
                tcount += 1
                kvf = KVf[i2]
                D("sp" if tcount % 2 else "act", kvf.a[:], bass.AP(cache, ((l * NSQ + s) * 2048 + row0) * 1024, [[st_ * 1024, 128], [1, 1024]]), w=[kvf])
                G(lambda e, kvf=kvf, i2=i2: e.tensor_copy(out=Kbc[i2].a[:], in_=kvf.a[:, 0:512]), [kvf], [Kbc[i2]])
                G(lambda e, kvf=kvf, i2=i2: e.tensor_copy(out=Vc[i2].a[:, 0:520].rearrange("p (h e) -> p h e", h=8)[:, :, 0:64],
                                                          in_=kvf.a[:, 512:1024].rearrange("p (h e) -> p h e", h=8)), [kvf], [Vc[i2]])
                p2 = nbk()
                for hp in range(4):
                    tr(p2.a[:, hp * 128:(hp + 1) * 128], Kbc[i2].a[:, hp * 128:(hp + 1) * 128], identb.a[:], [Kbc[i2], identb], [p2])
                evac(KTc[i2].a[:], p2.a[:, 0:512], [p2], [KTc[i2]])
                ps = nf()
                for h in range(8):
                    mm(ps.a[:, h * 4:(h + 1) * 4], KTc[i2].a[:, (h // 2) * 128:(h // 2 + 1) * 128], QTm.a[:, h * 64 + 4 * s:h * 64 + 4 * s + 4], True, False, [KTc[i2], QTm], [ps])
                    bo = (ty * 8 + h) * 4
                    mm(ps.a[:, h * 4:(h + 1) * 4], identb.a[:], SBt.a[:, bo:bo + 4], False, True, [identb, SBt], [ps])
                A(lambda e, ps=ps, s=s: e.activation(out=PTs.a[:, s * 512:(s + 1) * 512].rearrange("p (h t) -> p h t", h=8)[:, :, 4 * s:4 * s + 4],
                                                     in_=ps.a[:, 0:32].rearrange("p (h t) -> p h t", h=8), func=AF.Exp), [ps], [PTs])
                poA = nf()
                poB = nf()
                for h in range(8):
                    po = poA if h < 4 else poB
                    c = (h % 4) * 65
                    mm(po.a[:TS, c:c + 65], PTs.a[:, s * 512 + h * 64:s * 512 + (h + 1) * 64], Vc[i2].a[:, h * 65:(h + 1) * 65], True, True, [PTs, Vc[i2]], [po])
                V(lambda e, poA=poA: e.tensor_tensor(out=acc.a[:TS, 0:260], in0=poA.a[:TS, 0:260], in1=acc.a[:TS, 0:260], op=ALU.add), [poA, acc], [acc])
                V(lambda e, poB=poB: e.tensor_tensor(out=acc.a[:TS, 260:520], in0=poB.a[:TS, 0:260], in1=acc.a[:TS, 260:520], op=ALU.add), [poB, acc], [acc])
        acc3 = acc.a[:TS, :].rearrange("p (h e) -> p h e", h=8)
        V(lambda e: e.reciprocal(out=rden.a[:TS, :].unsqueeze(2), in_=acc3[:, :, 64:65]), [acc], [rden])
        V(lambda e: e.tensor_tensor(out=oas.a[:TS, :].rearrange("p (h e) -> p h e", h=8), in0=acc3[:, :, 0:64],
                                    in1=rden.a[:TS, :].unsqueeze(2).to_broadcast([TS, 8, 64]), op=ALU.mult), [acc, rden], [oas])
        D("sp", OAS.ap()[0:TS, :], oas.a[:TS, :], r=[oas], w=[dummy()])
        D("sp", OAS.ap()[TS:128, :], oas.a[:TS, :], r=[oas], w=[dummy()])

        S.barrier()
        reset_arena()
        w_out_v = wbig.a[:, 0:8192].rearrange("p (k c) -> p k c", k=8)
        w_pq_v = wbig.a[:, 8192:8192 + 16384].rearrange("p (k c) -> p k c", k=8)
        load_weight(w_out, l, 1024, w_out_v)
        load_weight(w_pq, l, 2048, w_pq_v)
        gffn = cf32(1024, "gffn")
        D("sp", gffn.a[:], g_ffn.ap()[l].partition_broadcast(128), w=[gffn])
        if l == 1:
            gfin = cf32(1024, "gfin")
            D("sp", gfin.a[:], g_fin.ap().partition_broadcast(128), w=[gfin])
        KEYS = cf32(256, "KEYS")
        D("sp", KEYS.a[:, :].rearrange("p (a k) -> p a k", a=2), keysT.ap()[l].rearrange("a c k -> c a k"), w=[KEYS])
        xt = cf32(1024, "xD")
        of = cf32(1024, "of")
        oa3 = cf32(3 * 520, "oa3")
        rden = cf32(8, "rdenD")
        ob = cb16(1024, "ob")
        oT = cb16(1024, "oT")
        hn2f = cf32(1024, "hn2f")
        hn2b = cb16(1024, "hn2b")
        hn2T = cb16(1024, "hn2T")
        qkTf = cf32(2048, "qkTf")
        sc = cf32(2048, "sc")
        sv = cf32(256, "sv")
        si = T(sb("si%d" % l, [128, 256], U32), "si")
        sif = cf32(256, "sif")
        wk = cf32(256, "wk")
        cand = cf32(256, "cand")
        rq = cf32(256, "rq")
        e12 = cf32(256, "e12")
        ri = T(sb("ri%d" % l, [128, 128], I32), "ri")
        ts_ = cf32(128, "ts")
        eq8 = cf32(2048, "eq8")
        eif = cf32(128, "eif")
        ei = T(sb("ei%d" % l, [128, 128], I32), "ei")
        pos = T(sb("pos%d" % l, [128, 128], U32), "pos")
        posf = cf32(128, "posf")
        gt = cf32(128, "gt")
        gs = cf32(8, "gs")
        pre = cf32(128, "pre")
        ge = cf32(128, "ge")
        aj = cf32(128, "aj")
        accv = cf32(1024, "accv")
        junkD = cf32(1024, "junkD")
        ssD = cf32(1, "ssD")
        NG = 5
        Gb = [cb16(2048, "G%d" % i) for i in range(NG)]
        diag = [cb16(128, "diag%d" % i) for i in range(4)]
        uvl = UVB[l].ap()
        for b in range(NB + 1):
            samp = (b == NB)
            if l == 0:
                src = xs.ap() if samp else xp.ap()[b * 128:(b + 1) * 128, :]
            else:
                src = XR.ap()[b * 128:(b + 1) * 128, :]
            D("sp", xt.a[:], src, w=[xt])
            if samp:
                D("sp", of.a[:, 0:512], OAS.ap(), w=[of])
            else:
                D("sp", oa3.a[:, :].rearrange("p (g c) -> p g c", g=3), bass.AP(OA, b * 128 * 520, [[520, 128], [NT * 520, 3], [1, 520]]), w=[oa3])
                V(lambda e: e.tensor_tensor(out=oa3.a[:, 0:520], in0=oa3.a[:, 0:520], in1=oa3.a[:, 520:1040], op=ALU.add), [oa3], [oa3])
                V(lambda e: e.tensor_tensor(out=oa3.a[:, 0:520], in0=oa3.a[:, 0:520], in1=oa3.a[:, 1040:1560], op=ALU.add), [oa3], [oa3])
                s3 = oa3.a[:, 0:520].rearrange("p (h e) -> p h e", h=8)
                V(lambda e, s3=s3: e.reciprocal(out=rden.a[:, :].unsqueeze(2), in_=s3[:, :, 64:65]), [oa3], [rden])
                V(lambda e, s3=s3: e.tensor_tensor(out=of.a[:, 0:512].rearrange("p (h e) -> p h e", h=8), in0=s3[:, :, 0:64],
                                                   in1=rden.a[:, :].unsqueeze(2).to_broadcast([128, 8, 64]), op=ALU.mult), [oa3, rden], [of])
            if samp:
                D("sp", of.a[0:TS, 512:1024], OBC.ap()[NT:NT + TS, :], w=[of])
                D("sp", of.a[TS:128, 512:1024], OBC.ap()[NT:NT + TS, :], w=[of])
            else:
                D("sp", of.a[:, 512:1024], OBC.ap()[b * 128:(b + 1) * 128, :], w=[of])
            A(lambda e: e.activation(out=ob.a[:], in_=of.a[:], func=AF.Copy), [of], [ob])
            transpose8(ob, oT)
            oTv = oT.a[:, :].rearrange("p (k t) -> p k t", k=8)
            for n in range(2):
                p = nf()
                for kc in range(8):
                    mm(p.a[:, :], oTv[:, kc, :], w_out_v[:, kc, n * 512:(n + 1) * 512], kc == 0, kc == 7, [oT, wbig], [p])
                V(lambda e, p=p, n=n: e.tensor_tensor(out=xt.a[:, n * 512:(n + 1) * 512], in0=p.a[:, :], in1=xt.a[:, n * 512:(n + 1) * 512], op=ALU.add), [p, xt], [xt])
            rmsnorm(xt, gffn, [(hn2f, hn2f.a[:])], junkD, ssD)
            A(lambda e: e.activation(out=hn2b.a[:], in_=hn2f.a[:], func=AF.Copy), [hn2f], [hn2b])
            transpose8(hn2b, hn2T)
            hTv = hn2T.a[:, :].rearrange("p (k t) -> p k t", k=8)
            for n in range(4):
                p = nf()
                for kc in range(8):
                    mm(p.a[:, :], hTv[:, kc, :], w_pq_v[:, kc, n * 512:(n + 1) * 512], kc == 0, kc == 7, [hn2T, wbig], [p])
                evac(z.a[:, n * 512:(n + 1) * 512], p.a[:, :], [p], [z])
            for n in range(4):
                p = nf()
                for j in range(4):
                    g_ = n * 4 + j
                    tr(p.a[:, j * 128:(j + 1) * 128], z.a[:, g_ * 128:(g_ + 1) * 128], identf, [z, cf], [p])
                evac(qkTf.a[:, n * 512:(n + 1) * 512], p.a[:, :], [p], [qkTf])
            for n in range(4):
                p = nf()
                for j in range(4):
                    g_ = n * 4 + j
                    mm(p.a[:, j * 128:(j + 1) * 128], qkTf.a[:, g_ * 128:(g_ + 1) * 128], KEYS.a[:, (g_ % 2) * 128:(g_ % 2 + 1) * 128], True, True, [qkTf, KEYS], [p])
                evac(sc.a[:, n * 512:(n + 1) * 512], p.a[:, :], [p], [sc])
            for g_ in range(16):
                scg = sc.a[:, g_ * 128:(g_ + 1) * 128]
                V(lambda e, g_=g_, scg=scg: e.max(out=sv.a[:, g_ * 16:g_ * 16 + 8], in_=scg), [sc], [sv])
                V(lambda e, g_=g_, scg=scg: e.max_index(out=si.a[:, g_ * 16:g_ * 16 + 8], in_max=sv.a[:, g_ * 16:g_ * 16 + 8], in_values=scg), [sc, sv], [si])
                V(lambda e, g_=g_, scg=scg: e.match_replace(out=wk.a[:, 0:128], in_to_replace=sv.a[:, g_ * 16:g_ * 16 + 8], in_values=scg, imm_value=-1e30), [sc, sv], [wk])
                V(lambda e, g_=g_: e.max(out=sv.a[:, g_ * 16 + 8:g_ * 16 + 16], in_=wk.a[:, 0:128]), [wk], [sv])
                V(lambda e, g_=g_: e.max_index(out=si.a[:, g_ * 16 + 8:g_ * 16 + 16], in_max=sv.a[:, g_ * 16 + 8:g_ * 16 + 16], in_values=wk.a[:, 0:128]), [wk, sv], [si])
            V(lambda e: e.tensor_copy(out=sif.a[:], in_=si.a[:]), [si], [sif])
            for h in range(8):
                a0 = (2 * h) * 16
                b0 = (2 * h + 1) * 16
                V(lambda e, a0=a0, b0=b0: e.tensor_tensor(out=cand.a[:, :].rearrange("p (r q) -> p r q", r=16),
                                                         in0=sv.a[:, a0:a0 + 16].unsqueeze(2).to_broadcast([128, 16, 16]),
                                                         in1=sv.a[:, b0:b0 + 16].unsqueeze(1).to_broadcast([128, 16, 16]), op=ALU.add), [sv], [cand])
                V(lambda e, h=h: e.max(out=ts_.a[:, h * 16:h * 16 + 8], in_=cand.a[:]), [cand], [ts_])
                V(lambda e, h=h: e.max_index(out=pos.a[:, h * 16:h * 16 + 8], in_max=ts_.a[:, h * 16:h * 16 + 8], in_values=cand.a[:]), [cand, ts_], [pos])
                V(lambda e, h=h: e.match_replace(out=wk.a[:], in_to_replace=ts_.a[:, h * 16:h * 16 + 8], in_values=cand.a[:], imm_value=-1e30), [cand, ts_], [wk])
                V(lambda e, h=h: e.max(out=ts_.a[:, h * 16 + 8:h * 16 + 16], in_=wk.a[:]), [wk], [ts_])
                V(lambda e, h=h: e.max_index(out=pos.a[:, h * 16 + 8:h * 16 + 16], in_max=ts_.a[:, h * 16 + 8:h * 16 + 16], in_values=wk.a[:]), [wk, ts_], [pos])
            V(lambda e: e.tensor_copy(out=posf.a[:], in_=pos.a[:]), [pos], [posf])
            V(lambda e: e.tensor_scalar(out=rq.a[:, 0:128], in0=posf.a[:], scalar1=0.0625, scalar2=-0.46875, op0=ALU.mult, op1=ALU.add), [posf], [rq])
            V(lambda e: e.tensor_copy(out=ri.a[:], in_=rq.a[:, 0:128]), [rq], [ri])
            V(lambda e: e.tensor_copy(out=rq.a[:, 0:128], in_=ri.a[:]), [ri], [rq])
            V(lambda e: e.scalar_tensor_tensor(out=rq.a[:, 128:256], in0=rq.a[:, 0:128], scalar=-16.0, in1=posf.a[:], op0=ALU.mult, op1=ALU.add), [rq, posf], [rq])
            sif4 = sif.a[:, :].rearrange("p (h a r) -> p h a r", h=8, a=2)
            oh3 = eq8.a[:, :].rearrange("p (j r) -> p j r", r=16)
            oh4 = eq8.a[:, :].rearrange("p (h k r) -> p h k r", h=8, k=16)
            for a_ in range(2):
                V(lambda e, a_=a_: e.tensor_tensor(out=oh3, in0=cf.a[:, C_IOTA:C_IOTA + 16].unsqueeze(1).to_broadcast([128, 128, 16]),
                                                   in1=rq.a[:, a_ * 128:(a_ + 1) * 128].unsqueeze(2).to_broadcast([128, 128, 16]), op=ALU.is_equal), [cf, rq], [eq8])
                V(lambda e, a_=a_: e.tensor_tensor(out=oh4, in0=oh4, in1=sif4[:, :, a_, :].unsqueeze(2).to_broadcast([128, 8, 16, 16]), op=ALU.mult), [eq8, sif], [eq8])
                V(lambda e, a_=a_: e.tensor_reduce(out=e12.a[:, a_ * 128:(a_ + 1) * 128], in_=oh3, axis=AX.X, op=ALU.add), [eq8], [e12])
            V(lambda e: e.scalar_tensor_tensor(out=eif.a[:], in0=e12.a[:, 0:128], scalar=128.0, in1=e12.a[:, 128:256], op0=ALU.mult, op1=ALU.add), [e12], [eif])
            ts3 = ts_.a[:, :].rearrange("p (h k) -> p h k", h=8)
            gt3 = gt.a[:, :].rearrange("p (h k) -> p h k", h=8)
            V(lambda e: e.tensor_tensor(out=gt3, in0=ts3, in1=ts3[:, :, 0:1].to_broadcast([128, 8, 16]), op=ALU.subtract), [ts_], [gt])
            A(lambda e: e.activation(out=gt.a[:], in_=gt.a[:], func=AF.Exp), [gt], [gt])
            V(lambda e: e.tensor_reduce(out=gs.a[:, :], in_=gt3, axis=AX.X, op=ALU.add), [gt], [gs])
            V(lambda e: e.reciprocal(out=gs.a[:], in_=gs.a[:]), [gs], [gs])
            V(lambda e: e.tensor_tensor(out=gt3, in0=gt3, in1=gs.a[:, :].unsqueeze(2).to_broadcast([128, 8, 16]), op=ALU.mult), [gt, gs], [gt])
            V(lambda e: e.tensor_copy(out=ei.a[:], in_=eif.a[:]), [eif], [ei])
            V(lambda e: e.memset(pre.a[:], 0.0), [], [pre])
            pacc = [nf(), nf()]
            for j in range(128):
                gb_ = Gb[j % NG]
                dg = diag[j % 4]
                S.dma("pool", lambda e, j=j, gb_=gb_: e.indirect_dma_start(
                    out=gb_.a[:, :], out_offset=None, in_=uvl,
                    in_offset=bass.IndirectOffsetOnAxis(ap=ei.a[:, j:j + 1], axis=0)),
                    [ei.r], [gb_.r])
                V(lambda e, j=j, gb_=gb_: e.scalar_tensor_tensor(out=junkD.a[:], in0=hn2f.a[:], scalar=1.0, in1=gb_.a[:, 0:1024],
                                                                op0=ALU.mult, op1=ALU.mult, accum_out=pre.a[:, j:j + 1]), [hn2f, gb_], [junkD, pre])
                A(lambda e, j=j: e.activation(out=ge.a[:, j:j + 1], in_=pre.a[:, j:j + 1], func=AF.Gelu), [pre], [ge])
                A(lambda e, j=j: e.activation(out=aj.a[:, j:j + 1], in_=ge.a[:, j:j + 1], func=AF.Copy, scale=gt.a[:, j:j + 1]), [ge, gt], [aj])
                A(lambda e, j=j, dg=dg: e.activation(out=dg.a[:], in_=identb.a[:], func=AF.Copy, scale=aj.a[:, j:j + 1]), [identb, aj], [dg])
                for hf in range(2):
                    mm(pacc[hf].a[:, :], dg.a[:], gb_.a[:, 1024 + hf * 512:1536 + hf * 512], j == 0, j == 127, [dg, gb_], [pacc[hf]])
            for hf in range(2):
                V(lambda e, hf=hf: e.tensor_tensor(out=xt.a[:, hf * 512:(hf + 1) * 512], in0=pacc[hf].a[:, :], in1=xt.a[:, hf * 512:(hf + 1) * 512], op=ALU.add), [pacc[hf], xt], [xt])
            if l == 0:
                D("sp", XR.ap()[b * 128:(b + 1) * 128, :], xt.a[:], r=[xt], w=[dummy()])
            else:
                rmsnorm(xt, gfin, [(of, of.a[:])], junkD, ssD)
                if samp:
                    D("sp", y_s.ap(), of.a[0:TS, :], r=[of], w=[dummy()])
                else:
                    D("sp", y_p.ap()[b * 128:(b + 1) * 128, :], of.a[:], r=[of], w=[dummy()])

    S.barrier()
    S.emit()
    S.close()
    es.close()
    return nc, S


_CACHE = {}


def run(inputs, SEQ, ncores, trace=False):
    NB = SEQ // 128
    x_prompt = np.asarray(inputs["x_prompt"], np.float32)
    x_sample = np.asarray(inputs["x_sample"], np.float32)
    cache_kv = np.asarray(inputs["cache_kv_win"], np.float32)
    state_gla = np.asarray(inputs["state_gla"], np.float32)
    state_ret = np.asarray(inputs["state_ret"], np.float32)
    nbatch = x_prompt.shape[0]
    c, bias, sbt, nbt, seqcol = make_consts()
    u_tab = np.asarray(inputs["u_tab"], np.float32)
    v_tab = np.asarray(inputs["v_tab"], np.float32)
    uvt = np.concatenate([u_tab, v_tab], axis=-1)
    keysT = np.ascontiguousarray(np.asarray(inputs["sub_keys"], np.float32).transpose(0, 1, 3, 2))
    common = {
        "w_in": np.asarray(inputs["w_in"], np.float32),
        "w_g2": np.asarray(inputs["w_gate2"], np.float32),
        "b_g": np.asarray(inputs["b_gate"], np.float32).reshape(2, 1, 128),
        "g_gla": np.asarray(inputs["g_gla"], np.float32),
        "w_out": np.asarray(inputs["w_out"], np.float32),
        "g_mix": np.asarray(inputs["g_mix"], np.float32),
        "g_ffn": np.asarray(inputs["g_ffn"], np.float32),
        "w_pq": np.asarray(inputs["w_pq"], np.float32),
        "keysT": keysT,
        "uv0": np.ascontiguousarray(uvt[0]), "uv1": np.ascontiguousarray(uvt[1]),
        "g_fin": np.asarray(inputs["g_final"], np.float32),
        "c_f32": c, "c_bias": bias, "c_sb": sbt, "c_nb": nbt, "c_sc": seqcol,
    }
    in_maps = []
    for ci in range(ncores):
        xs = x_sample[ci * NSQ:(ci + 1) * NSQ].reshape(TS, DM)
        xs2 = np.concatenate([xs, xs], axis=0)
        m = dict(common)
        m["xp"] = np.ascontiguousarray(x_prompt[ci % nbatch])
        m["xs"] = np.ascontiguousarray(xs2)
        m["cache"] = np.ascontiguousarray(cache_kv[:, ci * NSQ:(ci + 1) * NSQ].reshape(2, NSQ, 2048, 1024))
        m["sgla"] = np.ascontiguousarray(state_gla[:, ci * NSQ:(ci + 1) * NSQ].reshape(2, NSQ, 128, 64))
        m["sret"] = np.ascontiguousarray(state_ret[:, ci * NSQ:(ci + 1) * NSQ].reshape(2, NSQ, 128, 64))
        in_maps.append(m)
    if NB not in _CACHE:
        _CACHE[NB] = build(NB)[0]
    nc = _CACHE[NB]
    res = run_bass_kernel_spmd(nc, in_maps, core_ids=list(range(ncores)), **({"trace": True} if trace else {}))
    R = res.results
    y_prompt = np.stack([R[b]["y_p"] for b in range(nbatch)], 0).reshape(nbatch, SEQ, DM)
    y_sample = np.concatenate([R[ci]["y_s"].reshape(NSQ, 4, DM) for ci in range(ncores)], 0)
    kv_p = np.stack([np.stack([R[b]["kvp"][l] for b in range(nbatch)], 0) for l in range(2)], 0).reshape(2, nbatch, 2048, 2, 8, 64)
    kv_s = np.stack([np.concatenate([R[ci]["kvs"][l].reshape(NSQ, 4, 2, 8, 64) for ci in range(ncores)], 0) for l in range(2)], 0)
    gla_p = np.stack([np.stack([R[b]["glap"][l] for b in range(nbatch)], 0) for l in range(2)], 0).reshape(2, nbatch, 4, 32, 64)
    ret_p = np.stack([np.stack([R[b]["retp"][l] for b in range(nbatch)], 0) for l in range(2)], 0).reshape(2, nbatch, 4, 32, 64)
    gla_s = np.stack([np.concatenate([R[ci]["glas"][l] for ci in range(ncores)], 0) for l in range(2)], 0).reshape(2, ncores * NSQ, 4, 32, 64)
    ret_s = np.stack([np.concatenate([R[ci]["rets"][l] for ci in range(ncores)], 0) for l in range(2)], 0).reshape(2, ncores * NSQ, 4, 32, 64)
    outs = (y_prompt, y_sample, kv_p, kv_s, gla_p, gla_s, ret_p, ret_s)
    return tuple(np.ascontiguousarray(o, dtype=np.float32) for o in outs)


def kernel(**inputs):
    return run(inputs, 8192, NCORES)
```
